# Optimizing a Trainium2 kernel written in Bass

```python
import math
import jax, jax.numpy as jnp
from jax import lax
import numpy as np

D_MODEL = 1024
BATCH = 8
SEQ = 2048
DEPTH = 2

HEAD_DIM = 64
BLK = 128
A_GROUPS = ((128, 1), (512, 4), (2048, 16))
A_HEADS = 4
B_Q_HEADS = 8
B_KV_HEADS = 2
B_WINDOW = 128
C_HEADS = 4
N_BRANCH = 3
NUM_BUCKETS = 32
MAX_DISTANCE = max(w for w, _ in A_GROUPS)
D_FF = 4 * D_MODEL
CONV_WIDTH = 3
EPS = 1e-6

A_WIDTH = A_HEADS * HEAD_DIM
B_WIDTH = B_Q_HEADS * HEAD_DIM
C_WIDTH = C_HEADS * HEAD_DIM
B_GROUP = B_Q_HEADS // B_KV_HEADS
N_A_GROUP_HEADS = len(A_GROUPS) * A_HEADS
N_BIAS_HEADS = N_A_GROUP_HEADS + B_Q_HEADS
A_QKV_COLS = 3 * N_A_GROUP_HEADS * HEAD_DIM
B_Q_COLS = B_WIDTH
B_KV_COLS = 2 * B_KV_HEADS * HEAD_DIM
C_QKV_COLS = 3 * C_WIDTH
GATE_COLS = N_BRANCH * D_MODEL
OFF_B_Q = A_QKV_COLS
OFF_B_KV = OFF_B_Q + B_Q_COLS
OFF_C = OFF_B_KV + B_KV_COLS
OFF_GATE = OFF_C + C_QKV_COLS
IN_COLS = OFF_GATE + GATE_COLS
SCALE = HEAD_DIM ** -0.5

kernel_name = 'hybrid_dilated_swa_stickbreak_convffn'


def rms_norm(x, g):
    xf = x.astype(jnp.float32)
    y = xf * lax.rsqrt(jnp.mean(xf * xf, axis=-1, keepdims=True) + EPS)
    return (y * g.astype(jnp.float32)).astype(x.dtype)


def t5_bucket(dist):
    max_exact = NUM_BUCKETS // 2
    nf = jnp.maximum(dist, 1).astype(jnp.float32)
    large = max_exact + (jnp.log(nf / max_exact) / math.log(MAX_DISTANCE / max_exact)
                         * (NUM_BUCKETS - max_exact)).astype(jnp.int32)
    large = jnp.minimum(large, NUM_BUCKETS - 1)
    return jnp.where(dist < max_exact, dist, large)


def band_layout(n_blocks, max_dist):
    a = jnp.arange(BLK)[:, None]
    b = jnp.arange(2 * BLK)[None, :]
    dist = a + BLK - b
    in_band = (dist >= 0) & (dist <= max_dist)
    key_exists = (jnp.arange(n_blocks)[:, None] > 0) | (jnp.arange(2 * BLK)[None, :] >= BLK)
    mask = in_band[None] & key_exists[:, None, :]
    return jnp.maximum(dist, 0), mask


def band_bias(table, dist):
    return jnp.transpose(table[t5_bucket(dist)], (2, 0, 1)).astype(jnp.float32)


def banded_attention(q, k, v, bias, mask):
    n, hkv, g, L, dh = q.shape
    nb = L // BLK
    qb = q.reshape(n, hkv, g, nb, BLK, dh)

    def two_blocks(t):
        tb = t.reshape(n, hkv, nb, BLK, dh)
        prev = jnp.pad(tb, ((0, 0), (0, 0), (1, 0), (0, 0), (0, 0)))[:, :, :nb]
        return jnp.concatenate([prev, tb], axis=3)

    kw, vw = two_blocks(k), two_blocks(v)
    logits = jnp.einsum('nkgbqd,nkbsd->nkgbqs', qb, kw, preferred_element_type=jnp.float32) * SCALE
    logits = jnp.where(mask, logits + bias[:, :, None], -jnp.inf)
    m = jnp.max(logits, axis=-1)
    p = jnp.exp(logits - m[..., None])
    l = jnp.sum(p, axis=-1)
    num = jnp.einsum('nkgbqs,nkbsd->nkgbqd', p, vw.astype(jnp.float32))
    return num.reshape(n, hkv, g, L, dh), m.reshape(n, hkv, g, L), l.reshape(n, hkv, g, L)


def to_sub(t, d, lp):
    b, s, h, dh = t.shape
    L = s // d
    t = t.reshape(b, L, d, h, dh).transpose(0, 2, 3, 1, 4).reshape(b * d, h, L, dh)
    return jnp.pad(t, ((0, 0), (0, 0), (0, lp - L), (0, 0)))


def dilated_attention(q, k, v, table_a):
    b, s = q.shape[:2]
    nums, ms, ls = [], [], []
    for gi, (window, d) in enumerate(A_GROUPS):
        L = s // d
        lp = -(-L // BLK) * BLK
        dist, mask = band_layout(lp // BLK, window // d)
        bias = band_bias(table_a[:, gi], dist * d)[:, None]
        num, m, l = banded_attention(to_sub(q[:, :, gi], d, lp)[:, :, None],
                                     to_sub(k[:, :, gi], d, lp), to_sub(v[:, :, gi], d, lp), bias, mask)
        num = num[:, :, 0, :L].reshape(b, d, A_HEADS, L, HEAD_DIM).transpose(0, 3, 1, 2, 4)
        nums.append(num.reshape(b, s, A_HEADS, HEAD_DIM))
        ms.append(m[:, :, 0, :L].reshape(b, d, A_HEADS, L).transpose(0, 3, 1, 2).reshape(b, s, A_HEADS))
        ls.append(l[:, :, 0, :L].reshape(b, d, A_HEADS, L).transpose(0, 3, 1, 2).reshape(b, s, A_HEADS))
    nums = jnp.stack(nums, axis=2)
    ms = jnp.stack(ms, axis=2)
    ls = jnp.stack(ls, axis=2)
    c = jnp.exp(ms - jnp.max(ms, axis=2, keepdims=True))
    out = jnp.sum(c[..., None] * nums, axis=2) / jnp.sum(c * ls, axis=2)[..., None]
    return out.reshape(b, s, A_WIDTH).astype(q.dtype)


def sliding_window_gqa(q, k, v, sinks, table_b):
    b, s = q.shape[:2]
    dist, mask = band_layout(s // BLK, B_WINDOW - 1)
    bias = band_bias(table_b, dist).reshape(B_KV_HEADS, B_GROUP, BLK, 2 * BLK)
    qh = q.reshape(b, s, B_KV_HEADS, B_GROUP, HEAD_DIM).transpose(0, 2, 3, 1, 4)
    num, m, l = banded_attention(qh, k.transpose(0, 2, 1, 3), v.transpose(0, 2, 1, 3), bias, mask)
    sink = sinks.reshape(B_KV_HEADS, B_GROUP)[None, :, :, None].astype(jnp.float32)
    mx = jnp.maximum(m, sink)
    c = jnp.exp(m - mx)
    out = num * (c / (l * c + jnp.exp(sink - mx)))[..., None]
    return out.transpose(0, 3, 1, 2, 4).reshape(b, s, B_WIDTH).astype(q.dtype)


def stick_breaking_attention(q, k, v):
    b, s = q.shape[:2]
    qh, kh, vh = (t.transpose(0, 2, 1, 3) for t in (q, k, v))
    outs = []
    for i in range(s // BLK):
        lo, hi = i * BLK, (i + 1) * BLK
        z = jnp.einsum('bhqd,bhsd->bhqs', qh[:, :, lo:hi], kh[:, :, :hi],
                       preferred_element_type=jnp.float32) * SCALE
        before = jnp.arange(hi)[None, :] < jnp.arange(lo, hi)[:, None]
        log_keep = jnp.where(before, jax.nn.log_sigmoid(-z), 0.0)
        log_rest = lax.cumsum(log_keep, axis=3, reverse=True) - log_keep
        wts = jnp.where(before, jnp.exp(jax.nn.log_sigmoid(z) + log_rest), 0.0)
        outs.append(jnp.einsum('bhqs,bhsd->bhqd', wts, vh[:, :, :hi].astype(jnp.float32)))
    out = jnp.concatenate(outs, axis=2).transpose(0, 2, 1, 3)
    return out.reshape(b, s, C_WIDTH).astype(q.dtype)


def hybrid_mixer(h, rel_bias, w_in, b_gate, sinks, w_br_a, w_br_b, w_br_c, w_out):
    b, s, _ = h.shape
    proj = h @ w_in
    a_qkv, b_q, b_kv, c_qkv, gates = jnp.split(proj, [OFF_B_Q, OFF_B_KV, OFF_C, OFF_GATE], axis=-1)
    a_qkv = a_qkv.reshape(b, s, 3, len(A_GROUPS), A_HEADS, HEAD_DIM)
    table_a = rel_bias[:, :N_A_GROUP_HEADS].reshape(NUM_BUCKETS, len(A_GROUPS), A_HEADS)
    o_a = dilated_attention(a_qkv[:, :, 0], a_qkv[:, :, 1], a_qkv[:, :, 2], table_a)
    b_kv = b_kv.reshape(b, s, 2, B_KV_HEADS, HEAD_DIM)
    o_b = sliding_window_gqa(b_q.reshape(b, s, B_Q_HEADS, HEAD_DIM), b_kv[:, :, 0], b_kv[:, :, 1],
                             sinks, rel_bias[:, N_A_GROUP_HEADS:])
    c_qkv = c_qkv.reshape(b, s, 3, C_HEADS, HEAD_DIM)
    o_c = stick_breaking_attention(c_qkv[:, :, 0], c_qkv[:, :, 1], c_qkv[:, :, 2])
    g = jax.nn.sigmoid(gates.reshape(b, s, N_BRANCH, D_MODEL) + b_gate)
    merged = g[:, :, 0] * (o_a @ w_br_a) + g[:, :, 1] * (o_b @ w_br_b) + g[:, :, 2] * (o_c @ w_br_c)
    return merged @ w_out


def conv_ffn(h, w_up, conv_w, conv_b, w_down):
    u = h @ w_up
    u = lax.conv_general_dilated(u, conv_w[:, None, :], window_strides=(1,),
                                 padding=[(CONV_WIDTH - 1, 0)],
                                 dimension_numbers=('NWC', 'WIO', 'NWC'),
                                 feature_group_count=u.shape[-1]) + conv_b
    gate, val = jnp.split(u, 2, axis=-1)
    return (jax.nn.gelu(gate, approximate=True) * val) @ w_down


def setup_inputs(seed: int = 0) -> dict:
    key = jax.random.key(seed)
    ks = jax.random.split(key, 17)

    def nrm(k, shape, scale):
        return jax.random.normal(k, shape, jnp.float32) * scale

    def gain(k):
        return 1.0 + nrm(k, (DEPTH, D_MODEL), 0.1)

    return {
        'x': nrm(ks[0], (BATCH, SEQ, D_MODEL), 1.0),
        'rel_bias': nrm(ks[1], (NUM_BUCKETS, N_BIAS_HEADS), 0.5),
        'attn_pre_norm': gain(ks[2]),
        'w_in': nrm(ks[3], (DEPTH, D_MODEL, IN_COLS), D_MODEL ** -0.5),
        'b_gate': nrm(ks[4], (DEPTH, N_BRANCH, D_MODEL), 0.1),
        'sinks': nrm(ks[5], (DEPTH, B_Q_HEADS), 0.5),
        'w_br_a': nrm(ks[6], (DEPTH, A_WIDTH, D_MODEL), A_WIDTH ** -0.5),
        'w_br_b': nrm(ks[7], (DEPTH, B_WIDTH, D_MODEL), B_WIDTH ** -0.5),
        'w_br_c': nrm(ks[8], (DEPTH, C_WIDTH, D_MODEL), C_WIDTH ** -0.5),
        'w_out': nrm(ks[9], (DEPTH, D_MODEL, D_MODEL), D_MODEL ** -0.5),
        'attn_post_norm': gain(ks[10]),
        'ffn_pre_norm': gain(ks[11]),
        'w_up': nrm(ks[12], (DEPTH, D_MODEL, 2 * D_FF), D_MODEL ** -0.5),
        'conv_w': nrm(ks[13], (DEPTH, CONV_WIDTH, 2 * D_FF), CONV_WIDTH ** -0.5),
        'conv_b': nrm(ks[14], (DEPTH, 2 * D_FF), 0.02),
        'w_down': nrm(ks[15], (DEPTH, D_FF, D_MODEL), D_FF ** -0.5),
        'ffn_post_norm': gain(ks[16]),
    }


def reference(x, rel_bias, attn_pre_norm, w_in, b_gate, sinks, w_br_a, w_br_b, w_br_c, w_out,
              attn_post_norm, ffn_pre_norm, w_up, conv_w, conv_b, w_down, ffn_post_norm):
    for layer in range(DEPTH):
        h = rms_norm(x, attn_pre_norm[layer])
        h = hybrid_mixer(h, rel_bias, w_in[layer], b_gate[layer], sinks[layer],
                         w_br_a[layer], w_br_b[layer], w_br_c[layer], w_out[layer])
        x = x + rms_norm(h, attn_post_norm[layer])
        h = rms_norm(x, ffn_pre_norm[layer])
        h = conv_ffn(h, w_up[layer], conv_w[layer], conv_b[layer], w_down[layer])
        x = x + rms_norm(h, ffn_post_norm[layer])
    return x
```

```python
import numpy as np
from contextlib import ExitStack
import ml_dtypes
import concourse.bass as bass
import concourse.mybir as mybir
from concourse.bass_utils import run_bass_kernel_spmd

F32 = mybir.dt.float32
BF16 = mybir.dt.bfloat16
AF = mybir.ActivationFunctionType
ALU = mybir.AluOpType

PE, ACT, DVE, POOL, SP = "pe", "act", "dve", "pool", "sp"
ENGS = (PE, ACT, DVE, POOL, SP)
N_DMA_SEMS = 10

S_LEN = 2048
D = 1024
DEPTH = 2
NT = 16
KC = 8
SCALE = 0.125
EPS = 1e-6
NEG = -30000.0
A_GROUPS = ((128, 1), (512, 4), (2048, 16))
OFF_B_Q = 2304
OFF_B_KV = 2816
OFF_C = 3072
OFF_GATE = 3840
IN_COLS = 6912
D_FF = 4096


class Sched:
    def __init__(self, nc, stack):
        self.nc = nc
        self.ops = {e: [] for e in ENGS}
        self.cnt = {e: 0 for e in ENGS}
        self.sem = {e: stack.enter_context(nc.semaphore("s_" + e)) for e in (PE, ACT, DVE, POOL)}
        self.dsem = {e: [stack.enter_context(nc.semaphore("d_%s%d" % (e, i))) for i in range(N_DMA_SEMS)]
                     for e in (SP, ACT, POOL)}
        self.dcnt = {e: [0] * N_DMA_SEMS for e in (SP, ACT, POOL)}
        self.drr = {e: 0 for e in (SP, ACT, POOL)}
        self.known = {e: {} for e in ENGS}
        self.lastw = {}
        self.readers = {}
        self.final_tokens = []

    def _need(self, eng, tok, waits):
        if tok is None:
            return
        sem, val, src = tok
        if eng == PE and src == PE:
            return
        k = id(sem)
        if self.known[eng].get(k, 0) >= val:
            return
        self.known[eng][k] = val
        for i, (s, v) in enumerate(waits):
            if s is sem:
                waits[i] = (s, max(v, val))
                return
        waits.append((sem, val))

    def _deps(self, eng, reads, writes):
        waits = []
        for k in reads:
            self._need(eng, self.lastw.get(k), waits)
        for k in writes:
            self._need(eng, self.lastw.get(k), waits)
            for t in self.readers.get(k, ()):
                self._need(eng, t, waits)
        return waits

    def _commit(self, tok, reads, writes):
        for k in reads:
            self.readers.setdefault(k, []).append(tok)
        for k in writes:
            self.lastw[k] = tok
            self.readers[k] = []

    def op(self, eng, fn, reads=(), writes=(), after=None):
        pw = [k for k in reads if k.startswith("ps")]
        if pw:
            reads = [k for k in reads if not k.startswith("ps")]
            writes = list(writes) + pw
        waits = self._deps(eng, reads, writes)
        if after is not None:
            self._need(eng, (after[0], after[1], "forced"), waits)
        self.cnt[eng] += 1
        tok = (self.sem[eng], self.cnt[eng], eng)
        self.ops[eng].append((waits, fn, self.sem[eng], 1))
        self._commit(tok, reads, writes)
        return tok

    def dma(self, q, fn, reads=(), writes=(), final=False):
        waits = self._deps(q, reads, writes)
        i = self.drr[q]
        self.drr[q] = (i + 1) % N_DMA_SEMS
        if self.dcnt[q][i] > 0:
            self._need(q, (self.dsem[q][i], self.dcnt[q][i], "dma"), waits)
        self.dcnt[q][i] += 16
        tok = (self.dsem[q][i], self.dcnt[q][i], "dma")
        self.ops[q].append((waits, fn, self.dsem[q][i], 16))
        self._commit(tok, reads, writes)
        if final:
            self.final_tokens.append(tok)
        return tok

    def emit(self, block, last=False):
        fw = []
        if last:
            for t in self.final_tokens:
                self._need(SP, t, fw)
        ops = self.ops
        self.ops = {e: [] for e in ENGS}

        def run(e, lst, extra=()):
            for waits, fn, sem, amt in lst:
                for s, v in waits:
                    e.wait_ge(s, v)
                fn(e).then_inc(sem, amt)
            for s, v in extra:
                e.wait_ge(s, v)

        @block.tensor
        def _(e):
            run(e, ops[PE])

        @block.scalar
        def _(e):
            run(e, ops[ACT])

        @block.vector
        def _(e):
            run(e, ops[DVE])

        @block.gpsimd
        def _(e):
            run(e, ops[POOL])

        @block.sync
        def _(e):
            run(e, ops[SP], fw)


class Rot:
    def __init__(self, items):
        self.items = list(items)
        self.i = 0

    def next(self):
        v = self.items[self.i % len(self.items)]
        self.i += 1
        return v


def build_program(dbg=None, stop=None):
    nc = bass.Bass("TRN2", target_bir_lowering=False)

    def din(name, shape, dt=F32):
        return nc.dram_tensor(name, list(shape), dt, kind="ExternalInput").ap()

    x_d = din("x", [S_LEN, D])
    w_in_d = din("w_in", [DEPTH, D, IN_COLS])
    w_br_d = din("w_br", [DEPTH, D, D])
    w_out_d = din("w_out", [DEPTH, D, D])
    w_up_d = din("w_up", [DEPTH, D, 2 * D_FF])
    w_down_d = din("w_down", [DEPTH, D_FF, D])
    gpa_d = din("gpre_a", [128, DEPTH * KC])
    gpf_d = din("gpre_f", [128, DEPTH * KC])
    gpost_a_d = din("gpost_a", [128, DEPTH, D])
    gpost_f_d = din("gpost_f", [128, DEPTH, D])
    bgate_d = din("bgate", [128, DEPTH * 24])
    sinks_d = din("sinks_b", [128, DEPTH * 8])
    convw_d = din("convw", [128, DEPTH * 3 * 64])
    convb_d = din("convb", [128, DEPTH * 64])
    braw_d = din("bias_raw", [128, 20 * 256])
    bmask_d = din("bias_mask", [128, 2 * 256])
    cmask01_d = din("cmask01", [128, 128])
    cmaskneg_d = din("cmaskneg", [128, 128])
    trineg_d = din("trineg8", [128, 128])
    onesneg_d = din("onesneg8", [128, 128])
    ident_d = din("ident", [128, 128])
    out_d = nc.dram_tensor("out", [S_LEN, D], F32, kind="ExternalOutput").ap()
    dbg_d = None
    if dbg is not None or stop is not None:
        dbg_d = nc.dram_tensor("dbg", [128, 8 * 2048], F32, kind="ExternalOutput").ap()

    with ExitStack() as top:
        S = Sched(nc, top)

        uid = [0]

        def sbt(st, name, shape, dt=F32):
            uid[0] += 1
            return st.enter_context(nc.sbuf_tensor("sb%d_%s" % (uid[0], name), list(shape), dt))

        ps = [top.enter_context(nc.psum_tensor("ps%d" % i, [128, 512], F32)) for i in range(8)]
        pk = ["ps%d" % i for i in range(8)]

        hT = sbt(top, "hT", [128, KC, S_LEN], BF16)
        ident = sbt(top, "ident", [128, 128], BF16)
        biasT = sbt(top, "biasT", [128, 20, 256], BF16)
        cmask01 = sbt(top, "cmask01", [128, 128], F32)
        cmaskneg = sbt(top, "cmaskneg", [128, 128], BF16)
        trineg = sbt(top, "trineg", [128, 128], BF16)
        onesneg = sbt(top, "onesneg", [128, 128], BF16)
        gpa = sbt(top, "gpa", [128, DEPTH * KC])
        gpf = sbt(top, "gpf", [128, DEPTH * KC])
        bgate = sbt(top, "bgate", [128, DEPTH * 24])
        esink = sbt(top, "esink", [128, DEPTH * 8])
        convw = sbt(top, "convw", [128, DEPTH * 3 * 64])
        convb = sbt(top, "convb", [128, DEPTH * 64])
        gpost = sbt(top, "gpost", [128, D])
        xa = [sbt(top, "xa%d" % i, [128, D]) for i in range(2)]
        xb1 = sbt(top, "xb", [128, D])
        xb = [xb1, xb1]
        xn = [sbt(top, "xn%d" % i, [128, D], BF16) for i in range(2)]
        sq = sbt(top, "sqjunk", [128, D], BF16)
        stat = [sbt(top, "stat%d" % i, [128, 8]) for i in range(2)]

        def mm(out, lhsT, rhs, start, stop, reads, writes, after=None):
            return S.op(PE, lambda e, o=out, l=lhsT, r=rhs, a=start, b=stop: e.matmul(o, lhsT=l, rhs=r, start=a, stop=b),
                        reads, writes, after=after)

        def act(out, in_, func, reads, writes, **kw):
            S.op(ACT, lambda e, o=out, i=in_, f=func, k=kw: e.activation(out=o, in_=i, func=f, **k), reads, writes)

        def copy_on(eng, out, in_, reads, writes):
            if eng == ACT:
                act(out, in_, AF.Copy, reads, writes)
            else:
                S.op(eng, lambda e, o=out, i=in_: e.tensor_copy(out=o, in_=i), reads, writes)

        def tt(eng, out, in0, in1, op, reads, writes):
            S.op(eng, lambda e, o=out, a=in0, b=in1, p=op: e.tensor_tensor(out=o, in0=a, in1=b, op=p), reads, writes)

        def ts(eng, out, in0, s1, s2, op0, op1, reads, writes):
            if s2 is None:
                S.op(eng, lambda e, o=out, a=in0, x=s1, p=op0: e.tensor_scalar(out=o, in0=a, scalar1=x, scalar2=None, op0=p),
                     reads, writes)
            else:
                S.op(eng, lambda e, o=out, a=in0, x=s1, y=s2, p=op0, q=op1:
                     e.tensor_scalar(out=o, in0=a, scalar1=x, scalar2=y, op0=p, op1=q), reads, writes)

        def stt(eng, out, in0, scalar, in1, op0, op1, reads, writes):
            S.op(eng, lambda e, o=out, a=in0, s=scalar, b=in1, p=op0, q=op1:
                 e.scalar_tensor_tensor(out=o, in0=a, scalar=s, in1=b, op0=p, op1=q), reads, writes)

        def dma(q, out, in_, reads, writes, final=False):
            S.dma(q, lambda e, o=out, i=in_: e.dma_start(out=o, in_=i), reads, writes, final=final)

        def wload(dst, src, reads, writes):
            dma(POOL, dst, src, reads, writes)

        def norm_to_hT(xt_ap, xt_key, gvec, gcol0, t, par):
            st_ = stat[par]
            sk = "stat%d" % par
            S.op(POOL, lambda e, o=st_[:, 0:4]: e.memset(o, 0.0), [], [sk])
            act(sq[:, :], xt_ap, AF.Square, [xt_key], ["sq", sk], accum_out=st_[:, 0:1])
            ts(DVE, st_[:, 1:2], st_[:, 0:1], 1.0 / D, EPS, ALU.mult, ALU.add, [sk], [sk])
            act(st_[:, 2:3], st_[:, 1:2], AF.Ln, [sk], [sk])
            act(st_[:, 3:4], st_[:, 2:3], AF.Exp, [sk], [sk], scale=-0.5)
            xk = "xn%d" % par
            ts(DVE, xn[par][:, :], xt_ap, st_[:, 3:4], None, ALU.mult, None, [xt_key, sk], [xk])
            for half in range(2):
                b = tbank.next()
                for c4 in range(4):
                    c = half * 4 + c4
                    mm(ps[b][:, c4 * 128:(c4 + 1) * 128], xn[par][:, c * 128:(c + 1) * 128], ident[:, :], True, True,
                       [xk, "ident"], [pk[b]])
                for c4 in range(4):
                    c = half * 4 + c4
                    eng = DVE if c4 % 2 == 0 else POOL
                    if eng == POOL:
                        eng = DVE
                    ts(eng, hT[:, c, t * 128:(t + 1) * 128], ps[b][:, c4 * 128:(c4 + 1) * 128],
                       gvec[:, gcol0 + c:gcol0 + c + 1], None, ALU.mult, None, [pk[b], "gvec"], ["hT%d" % c])

        def post_residual(t, par, ys, x_src, gnext, gcol0, final):
            rows = slice(t * 128, (t + 1) * 128)
            st_ = stat[par]
            sk = "stat%d" % par
            xk = "xa%d" % par
            dma(SP, xa[par][:, :], x_src[rows, :], ["outd%d" % t], [xk])
            S.op(POOL, lambda e, o=st_[:, 4:8]: e.memset(o, 0.0), [], [sk])
            for i, (yap, ykey, csl) in enumerate(ys):
                n = csl.stop - csl.start
                act(sq[:, 0:n], yap, AF.Square, [ykey], ["sq", sk], accum_out=st_[:, 4 + i:5 + i])
            tt(DVE, st_[:, 6:7], st_[:, 4:5], st_[:, 5:6], ALU.add, [sk], [sk])
            ts(DVE, st_[:, 6:7], st_[:, 6:7], 1.0 / D, EPS, ALU.mult, ALU.add, [sk], [sk])
            act(st_[:, 7:8], st_[:, 6:7], AF.Ln, [sk], [sk])
            act(st_[:, 6:7], st_[:, 7:8], AF.Exp, [sk], [sk], scale=-0.5)
            bk = "xb"
            for i, (yap, ykey, csl) in enumerate(ys):
                stt(DVE, xb[par][:, csl], yap, st_[:, 6:7], gpost[:, csl], ALU.mult, ALU.mult, [ykey, sk, "gpost"], [bk])
            tt(POOL, xa[par][:, :], xa[par][:, :], xb[par][:, :], ALU.add, [xk, bk], [xk])
            dma(SP, out_d[rows, :], xa[par][:, :], [xk], ["outd%d" % t], final=final)
            if not final:
                norm_to_hT(xa[par][:, :], xk, gnext, gcol0, t, par)

        tbank = Rot([4, 5, 6, 7])

        with ExitStack() as ph:
            braw = sbt(ph, "braw", [128, 20, 256])
            bmask = sbt(ph, "bmask", [128, 2, 256])
            sinks = sbt(ph, "sinks", [128, DEPTH * 8])
            blk = ph.enter_context(nc.Block())
            dma(SP, braw[:, :, :], braw_d.rearrange("p (h n) -> p h n", h=20), [], ["braw"])
            dma(SP, bmask[:, :, :], bmask_d.rearrange("p (h n) -> p h n", h=2), [], ["bmask"])
            dma(SP, sinks[:, :], sinks_d, [], ["sinks"])
            dma(SP, cmask01[:, :], cmask01_d, [], ["cmask01"])
            for dst, src, k in ((cmaskneg, cmaskneg_d, "cmaskneg"), (trineg, trineg_d, "trineg"),
                                (onesneg, onesneg_d, "onesneg"), (ident, ident_d, "ident")):
                wload(dst[:, :], src, [], [k])
            dma(SP, gpa[:, :], gpa_d, [], ["gvec"])
            dma(SP, gpf[:, :], gpf_d, [], ["gvec"])
            dma(SP, bgate[:, :], bgate_d, [], ["bgate"])
            dma(SP, convw[:, :], convw_d, [], ["convw"])
            dma(SP, convb[:, :], convb_d, [], ["convb"])
            act(esink[:, :], sinks[:, :], AF.Exp, ["sinks"], ["esink"])
            for h in range(20):
                mi = 0 if h < 12 else 1
                stt(DVE, biasT[:, h, :], braw[:, h, :], 1.0 / SCALE, bmask[:, mi, :], ALU.mult, ALU.add,
                    ["braw", "bmask"], ["biasT"])
            for t in range(NT):
                par = t % 2
                dma(SP, xa[par][:, :], x_d[t * 128:(t + 1) * 128, :], [], ["xa%d" % par])
                norm_to_hT(xa[par][:, :], "xa%d" % par, gpa, 0, t, par)
            if stop == "p0":
                for c in range(KC):
                    dma(POOL, dbg_d[:, c * 2048:(c + 1) * 2048], hT[:, c, :], ["hT%d" % c], [], final=True)
            S.emit(blk, last=(stop == "p0"))
        if stop == "p0":
            return nc

        for l in range(DEPTH):
            with ExitStack() as lay:
                oT = sbt(lay, "oT", [128, 8, S_LEN], BF16)
                hkeys = ["hT%d" % c for c in range(KC)]

                with ExitStack() as ph:
                    QT = [sbt(ph, "QT%d" % i, [128, S_LEN], BF16) for i in range(2)]
                    KT = [sbt(ph, "KT%d" % i, [128, S_LEN], BF16) for i in range(2)]
                    V = [sbt(ph, "V%d" % i, [128, NT, 2, 128], BF16) for i in range(2)]
                    Wt = [sbt(ph, "Wt%d" % i, [128, KC, 384], BF16) for i in range(2)]
                    Pb = [sbt(ph, "Pb%d" % i, [128, 4096], BF16) for i in range(2)]
                    acc = sbt(ph, "acc", [128, 2, S_LEN], F32)
                    rec = [sbt(ph, "rec%d" % i, [128, 512], F32) for i in range(2)]
                    Eb = [sbt(ph, "Eb%d" % i, [128, 512], F32) for i in range(2)]
                    SPb = [sbt(ph, "SPb%d" % i, [128, 512], BF16) for i in range(3)]
                    Wb = [sbt(ph, "Wb%d" % i, [128, 512], BF16) for i in range(3)]
                    RS32 = sbt(ph, "RS32", [128, 512], F32)
                    RSb = [sbt(ph, "RSb%d" % i, [128, 512], BF16) for i in range(2)]
                    blk = ph.enter_context(nc.Block())
                    for i in range(2):
                        S.op(POOL, lambda e, v=V[i]: e.memset(v[:, :, :, 64:128], 1.0), [], ["V%d" % i])

                    sbank = Rot([0, 1])
                    obank = Rot([2, 3])
                    pbank = Rot([4, 5, 6, 7])
                    evq = Rot([DVE, ACT])

                    def proj(job):
                        si = job["set"]
                        wt = Wt[si]
                        wk = "Wt%d" % si
                        src = w_in_d[l]
                        c0 = job["qcol"]
                        wload(wt[:, :, 0:128], src[:, c0:c0 + 128].rearrange("(kc p) c -> p kc c", p=128), [], [wk])
                        for (cs, n, dst) in job["kcols"]:
                            wload(wt[:, :, 128 + dst:128 + dst + n],
                                  src[:, cs:cs + n].rearrange("(kc p) c -> p kc c", p=128), [], [wk])
                        cs, nv = job["vcols"]
                        wload(wt[:, :, 256:256 + nv], src[:, cs:cs + nv].rearrange("(kc p) c -> p kc c", p=128), [], [wk])
                        d = job["d"]
                        L = S_LEN // d
                        nb = L // 128
                        for which, dst, dk in ((0, QT[si], "QT%d" % si), (1, KT[si], "KT%d" % si)):
                            for tb in range(4):
                                b = pbank.next()
                                for kc in range(KC):
                                    if d == 1:
                                        rhs = hT[:, kc, tb * 512:(tb + 1) * 512]
                                        o = ps[b][:, :]
                                    elif d == 4:
                                        rhs = hT[:, kc, tb:S_LEN:4]
                                        o = ps[b][:, :]
                                    else:
                                        rhs = hT[:, kc, :].rearrange("p (m r) -> p r m", r=16)[:, 4 * tb:4 * tb + 4, :]
                                        o = ps[b][:, :].rearrange("p (r m) -> p r m", r=4)
                                    mm(o, wt[:, kc, which * 128:(which + 1) * 128], rhs, kc == 0, kc == KC - 1,
                                       [wk, "hT%d" % kc], [pk[b]])
                                copy_on(evq.next(), dst[:, tb * 512:(tb + 1) * 512], ps[b][:, :], [pk[b]], [dk])
                        vk = "V%d" % si
                        for t4 in range(4):
                            b = pbank.next()
                            for k in range(4):
                                ti = t4 * 4 + k
                                c, j = divmod(ti, nb)
                                start = j * 128 * d + c
                                for kc in range(KC):
                                    mm(ps[b][:, k * 128:k * 128 + nv], hT[:, kc, start:start + 127 * d + 1:d],
                                       wt[:, kc, 256:256 + nv], kc == 0, kc == KC - 1, [wk, "hT%d" % kc], [pk[b]])
                            if nv == 128:
                                o = V[si][:, t4 * 4:t4 * 4 + 4, :, 0:64]
                                i_ = ps[b][:, :].rearrange("p (t h e) -> p t h e", t=4, h=2)
                            else:
                                o = V[si][:, t4 * 4:t4 * 4 + 4, 0, 0:64]
                                i_ = ps[b][:, :].rearrange("p (t e) -> p t e", t=4)[:, :, 0:64]
                            copy_on(evq.next(), o, i_, [pk[b]], [vk])

                    def s_stage(job, head):
                        si = job["set"]
                        hl, hv, bi, pbi = head["hl"], head["hv"], head["bias"], head["pb"]
                        d = job["d"]
                        nb = (S_LEN // d) // 128
                        Wd = 256 if nb > 1 else 128
                        per = 512 // Wd
                        p0 = hl * 64
                        for u in range(NT // per):
                            b = sbank.next()
                            for k in range(per):
                                ti = u * per + k
                                c, j = divmod(ti, nb)
                                n = 256 if (nb > 1 and j < nb - 1) else 128
                                mm(ps[b][:, k * Wd:k * Wd + n], KT[si][p0:p0 + 64, ti * 128:(ti + 1) * 128],
                                   QT[si][p0:p0 + 64, ti * 128:ti * 128 + n], True, False,
                                   ["KT%d" % si, "QT%d" % si], [pk[b]])
                                mm(ps[b][:, k * Wd:k * Wd + n], ident[:, :], biasT[:, bi, 0:n], False, True,
                                   ["ident", "biasT"], [pk[b]])
                            act(Pb[pbi][:, u * 512:(u + 1) * 512], ps[b][:, :], AF.Exp, [pk[b]], ["Pb%d" % pbi],
                                scale=SCALE)

                    def pv_stage(job, head):
                        si = job["set"]
                        hl, hv, pbi = head["hl"], head["hv"], head["pb"]
                        d = job["d"]
                        nb = (S_LEN // d) // 128
                        Wd = 256 if nb > 1 else 128
                        for q4 in range(4):
                            b = obank.next()
                            for k in range(4):
                                qi = q4 * 4 + k
                                c, i = divmod(qi, nb)
                                lst = []
                                if i > 0:
                                    lst.append((qi - 1, 128))
                                lst.append((qi, 0))
                                for idx, (ti, off) in enumerate(lst):
                                    mm(ps[b][:, k * 128:(k + 1) * 128], V[si][:, ti, hv, :],
                                       Pb[pbi][:, ti * Wd + off:ti * Wd + off + 128], idx == 0, idx == len(lst) - 1,
                                       ["V%d" % si, "Pb%d" % pbi], [pk[b]])
                            head["evac"](job, head, q4, b)

                    def evac_A(job, head, q4, b):
                        hl = head["hl"]
                        d = job["d"]
                        gi = job["gi"]
                        ak = "acc%d" % hl
                        if d == 1:
                            o = acc[:, hl, q4 * 512:(q4 + 1) * 512]
                            i_ = ps[b][:, :]
                        elif d == 4:
                            o = acc[:, hl, q4:S_LEN:4]
                            i_ = ps[b][:, :]
                        else:
                            o = acc[:, hl, :].rearrange("p (m r) -> p r m", r=16)[:, 4 * q4:4 * q4 + 4, :]
                            i_ = ps[b][:, :].rearrange("p (r m) -> p r m", r=4)
                        if gi == 0:
                            copy_on(DVE, o, i_, [pk[b]], [ak])
                        else:
                            tt(DVE, o, i_, o, ALU.add, [pk[b], ak], [ak])

                    def finalize_A(pair):
                        for hl in range(2):
                            ak = "acc%d" % hl
                            for tb in range(4):
                                r = tb % 2
                                sl = slice(tb * 512, (tb + 1) * 512)
                                S.op(DVE, lambda e, o=rec[r][0:64, :], i=acc[64:128, hl, sl]: e.reciprocal(out=o, in_=i),
                                     [ak], ["rec%d" % r])
                                tt(POOL, oT[hl * 64:(hl + 1) * 64, pair, sl], acc[0:64, hl, sl], rec[r][0:64, :], ALU.mult,
                                   [ak, "rec%d" % r], ["oT%d" % pair])

                    def evac_B(job, head, q4, b):
                        hl = head["hl"]
                        h = head["h"]
                        r = q4 % 2
                        sl = slice(q4 * 512, (q4 + 1) * 512)
                        col = l * 8 + h
                        ts(DVE, rec[r][64:128, :], ps[b][64:128, :], esink[64:128, col:col + 1], None, ALU.add, None,
                           [pk[b], "esink"], ["rec%d" % r])
                        S.op(DVE, lambda e, o=rec[r][64:128, :]: e.reciprocal(out=o, in_=o), ["rec%d" % r], ["rec%d" % r])
                        ch = 2 + h // 2
                        tt(DVE, oT[hl * 64:(hl + 1) * 64, ch, sl], ps[b][0:64, :], rec[r][64:128, :], ALU.mult,
                           [pk[b], "rec%d" % r], ["oT%d" % ch])

                    jobs = []
                    for pair in range(2):
                        for gi, (win, d) in enumerate(A_GROUPS):
                            base = gi * 256 + pair * 128
                            jobs.append(dict(kind="A", set=len(jobs) % 2, qcol=base, kcols=[(768 + base, 128, 0)],
                                             vcols=(1536 + base, 128), d=d, gi=gi, pair=pair,
                                             heads=[dict(hl=hl, hv=hl, bias=gi * 4 + pair * 2 + hl, evac=evac_A)
                                                    for hl in range(2)]))
                    for pj in range(4):
                        kvh = pj // 2
                        kc0 = OFF_B_KV + kvh * 64
                        jobs.append(dict(kind="B", set=len(jobs) % 2, qcol=OFF_B_Q + pj * 128,
                                         kcols=[(kc0, 64, 0), (kc0, 64, 64)], vcols=(OFF_B_KV + 128 + kvh * 64, 64),
                                         d=1, gi=0, pair=pj,
                                         heads=[dict(hl=hl, hv=0, h=pj * 2 + hl, bias=12 + pj * 2 + hl, evac=evac_B)
                                                for hl in range(2)]))
                    cjobs = []
                    for pair in range(2):
                        cjobs.append(dict(kind="C", set=(len(jobs) + pair) % 2, qcol=OFF_C + pair * 128,
                                          kcols=[(OFF_C + 256 + pair * 128, 128, 0)],
                                          vcols=(OFF_C + 512 + pair * 128, 128), d=1, gi=0, pair=pair, heads=[]))
                    alljobs = jobs + cjobs

                    seq = []
                    for ji, job in enumerate(jobs):
                        for hi, head in enumerate(job["heads"]):
                            head["pb"] = len(seq) % 2
                            seq.append((ji, hi))
                    proj(alljobs[0])
                    prev = None
                    for (ji, hi) in seq:
                        job = jobs[ji]
                        head = job["heads"][hi]
                        s_stage(job, head)
                        if prev is not None:
                            pj_, ph_ = prev
                            pv_stage(jobs[pj_], jobs[pj_]["heads"][ph_])
                            if ph_ == 1 and jobs[pj_]["kind"] == "A" and jobs[pj_]["gi"] == 2:
                                finalize_A(jobs[pj_]["pair"])
                        if hi == 0 and ji + 1 < len(alljobs):
                            proj(alljobs[ji + 1])
                        prev = (ji, hi)
                    pj_, ph_ = prev
                    pv_stage(jobs[pj_], jobs[pj_]["heads"][ph_])
                    proj(cjobs[1])

                    cset = [cjobs[0]["set"], cjobs[1]["set"]]
                    zb = Rot([0, 1])
                    ab = Rot([2, 3])
                    tiles = [(qb, kb) for qb in range(NT) for kb in range(qb, -1, -1)]
                    info = {}

                    def c_st1(t):
                        qb, kb = tiles[t]
                        b = zb.next()
                        ptok = None
                        for h in (0, 2, 1, 3):
                            si = cset[h // 2]
                            p0 = (h % 2) * 64
                            ptok = mm(ps[b][:, h * 128:(h + 1) * 128], KT[si][p0:p0 + 64, kb * 128:(kb + 1) * 128],
                                      QT[si][p0:p0 + 64, qb * 128:(qb + 1) * 128], True, True,
                                      ["KT%d" % si, "QT%d" % si], [pk[b]], after=(ptok if h == 1 else None))
                        e_i = t % 2
                        sp_i = t % 3
                        act(Eb[e_i][:, :], ps[b][:, :], AF.Exp, [pk[b]], ["Eb%d" % e_i], scale=SCALE)
                        if kb == qb:
                            ev = Eb[e_i][:, :].rearrange("p (h n) -> p h n", h=4)
                            S.op(POOL, lambda e, o=ev, m=cmask01[:, :].unsqueeze(1).to_broadcast([128, 4, 128]):
                                 e.tensor_tensor(out=o, in0=o, in1=m, op=ALU.mult), ["Eb%d" % e_i, "cmask01"], ["Eb%d" % e_i])
                        act(SPb[sp_i][:, :], Eb[e_i][:, :], AF.Ln, ["Eb%d" % e_i], ["SPb%d" % sp_i], bias=1.0, scale=1.0)
                        info[t] = dict(sp=sp_i)

                    def c_st2(t):
                        qb, kb = tiles[t]
                        b = ab.next()
                        sp_i = info[t]["sp"]
                        rs_i = info[t - 1]["rs"] if kb < qb else None
                        for h in range(4):
                            si = cset[h // 2]
                            p0 = (h % 2) * 64
                            o = ps[b][:, h * 128:(h + 1) * 128]
                            hs = slice(h * 128, (h + 1) * 128)
                            last = "tri"
                            if kb < qb:
                                last = "ones"
                            if kb == qb:
                                last = "mask"
                            mm(o, KT[si][p0:p0 + 64, kb * 128:(kb + 1) * 128], QT[si][p0:p0 + 64, qb * 128:(qb + 1) * 128],
                               True, False, ["KT%d" % si, "QT%d" % si], [pk[b]])
                            mm(o, trineg[:, :], SPb[sp_i][:, hs], False, last == "tri", ["trineg", "SPb%d" % sp_i], [pk[b]])
                            if kb < qb:
                                mm(o, onesneg[:, :], RSb[rs_i][:, hs], False, True, ["onesneg", "RSb%d" % rs_i], [pk[b]])
                            if kb == qb:
                                mm(o, ident[:, :], cmaskneg[:, :], False, True, ["ident", "cmaskneg"], [pk[b]])
                        w_i = t % 3
                        act(Wb[w_i][:, :], ps[b][:, :], AF.Exp, [pk[b]], ["Wb%d" % w_i], scale=SCALE)
                        info[t]["w"] = w_i
                        if kb > 0:
                            if kb == qb:
                                copy_on(DVE, RS32[:, :], SPb[sp_i][:, :], ["SPb%d" % sp_i], ["RS32"])
                            else:
                                tt(DVE, RS32[:, :], RS32[:, :], SPb[sp_i][:, :], ALU.add, ["RS32", "SPb%d" % sp_i], ["RS32"])
                            r_i = t % 2
                            copy_on(POOL, RSb[r_i][:, :], RS32[:, :], ["RS32"], ["RSb%d" % r_i])
                            info[t]["rs"] = r_i

                    def c_st3(t):
                        qb, kb = tiles[t]
                        w_i = info[t]["w"]
                        ob = 4 + (qb % 2) * 2
                        for h in range(4):
                            si = cset[h // 2]
                            hl = h % 2
                            b = ob + h // 2
                            mm(ps[b][hl * 64:(hl + 1) * 64, 0:128], V[si][:, kb, hl, 0:64], Wb[w_i][:, h * 128:(h + 1) * 128],
                               kb == qb, kb == 0, ["V%d" % si, "Wb%d" % w_i], [pk[b]])
                        if kb == 0:
                            for pr in range(2):
                                b = ob + pr
                                copy_on(DVE if pr == 0 else ACT, oT[:, 6 + pr, qb * 128:(qb + 1) * 128], ps[b][:, 0:128],
                                        [pk[b]], ["oT%d" % (6 + pr)])

                    ntl = len(tiles)
                    for i in range(ntl + 2):
                        if i < ntl:
                            c_st1(i)
                        if 0 <= i - 1 < ntl:
                            c_st2(i - 1)
                        if 0 <= i - 2 < ntl:
                            c_st3(i - 2)
                    if stop == "p1":
                        for c in range(8):
                            dma(POOL, dbg_d[:, c * 2048:(c + 1) * 2048], oT[:, c, :], ["oT%d" % c], [], final=True)
                    S.emit(blk, last=(stop == "p1"))
                if stop == "p1":
                    return nc

                with ExitStack() as ph:
                    mT = sbt(ph, "mT", [128, KC, S_LEN], BF16)
                    Wg = [sbt(ph, "Wg%d" % i, [128, KC, 384], BF16) for i in range(2)]
                    Wbr = sbt(ph, "Wbr", [128, KC, D], BF16)
                    Wo = sbt(ph, "Wo", [128, KC, D], BF16)
                    sg = [sbt(ph, "sg%d" % i, [128, 512]) for i in range(3)]
                    tmpm = [sbt(ph, "tmpm%d" % i, [128, 512]) for i in range(3)]
                    blk = ph.enter_context(nc.Block())
                    wload(Wbr[:, :, :], w_br_d[l].rearrange("(kc p) c -> p kc c", p=128), [], ["Wbr"])
                    wload(Wo[:, :, :], w_out_d[l].rearrange("(kc p) c -> p kc c", p=128), [], ["Wo"])
                    dma(SP, gpost[:, :], gpost_a_d[:, l, :], [], ["gpost"])
                    gbank = Rot([0, 1, 2])
                    bbank = Rot([3, 4, 5])
                    brch = ((0, 2), (2, 6), (6, 8))
                    ui = 0
                    for dc in range(KC):
                        wi = dc % 2
                        wk = "Wg%d" % wi
                        for br in range(3):
                            c0 = OFF_GATE + br * 1024 + dc * 128
                            wload(Wg[wi][:, :, br * 128:(br + 1) * 128],
                                  w_in_d[l][:, c0:c0 + 128].rearrange("(kc p) c -> p kc c", p=128), [], [wk])
                        for tb in range(4):
                            tsl = slice(tb * 512, (tb + 1) * 512)
                            for br in range(3):
                                gb = gbank.next()
                                for kc in range(KC):
                                    mm(ps[gb][:, :], Wg[wi][:, kc, br * 128:(br + 1) * 128], hT[:, kc, tsl], kc == 0,
                                       kc == KC - 1, [wk, "hT%d" % kc], [pk[gb]])
                                bb = bbank.next()
                                c_lo, c_hi = brch[br]
                                for ch in range(c_lo, c_hi):
                                    mm(ps[bb][:, :], Wbr[:, ch, dc * 128:(dc + 1) * 128], oT[:, ch, tsl], ch == c_lo,
                                       ch == c_hi - 1, ["Wbr", "oT%d" % ch], [pk[bb]])
                                si_ = ui % 3
                                ui += 1
                                bcol = l * 24 + br * 8 + dc
                                act(sg[si_][:, :], ps[gb][:, :], AF.Sigmoid, [pk[gb], "bgate"], ["sg%d" % si_],
                                    bias=bgate[:, bcol:bcol + 1], scale=1.0)
                                if br == 0:
                                    tt(DVE, tmpm[0][:, :], ps[bb][:, :], sg[si_][:, :], ALU.mult, [pk[bb], "sg%d" % si_], ["tm0"])
                                elif br == 1:
                                    tt(DVE, tmpm[1][:, :], ps[bb][:, :], sg[si_][:, :], ALU.mult, [pk[bb], "sg%d" % si_], ["tm1"])
                                    tt(POOL, tmpm[0][:, :], tmpm[0][:, :], tmpm[1][:, :], ALU.add, ["tm0", "tm1"], ["tm0"])
                                else:
                                    tt(DVE, tmpm[2][:, :], ps[bb][:, :], sg[si_][:, :], ALU.mult, [pk[bb], "sg%d" % si_], ["tm2"])
                                    tt(POOL, mT[:, dc, tsl], tmpm[0][:, :], tmpm[2][:, :], ALU.add, ["tm0", "tm2"], ["mT%d" % dc])
                    ybank = Rot([6, 7])
                    x_src = x_d if l == 0 else out_d
                    for t in range(NT):
                        par = t % 2
                        rows = slice(t * 128, (t + 1) * 128)
                        ys = []
                        for half in range(2):
                            yb = ybank.next()
                            for kc in range(KC):
                                mm(ps[yb][:, :], mT[:, kc, rows], Wo[:, kc, half * 512:(half + 1) * 512], kc == 0, kc == KC - 1,
                                   ["mT%d" % kc, "Wo"], [pk[yb]])
                            ys.append((ps[yb][:, :], pk[yb], slice(half * 512, (half + 1) * 512)))
                        post_residual(t, par, ys, x_src, gpf, l * KC, False)
                    if stop == "p2":
                        for c in range(KC):
                            dma(POOL, dbg_d[:, c * 2048:(c + 1) * 2048], hT[:, c, :], ["hT%d" % c], [], final=True)
                    S.emit(blk, last=(stop == "p2"))
                if stop == "p2":
                    return nc

            with ExitStack() as ph:
                Y = sbt(ph, "Y", [128, NT, D], F32)
                aT = sbt(ph, "aT", [128, 4, S_LEN], BF16)
                Wu = [sbt(ph, "Wu%d" % i, [128, KC, 256], BF16) for i in range(2)]
                Wd = [sbt(ph, "Wd0", [128, 4, D], BF16)]
                ur = [sbt(ph, "ur%d" % i, [128, 2 + S_LEN], F32) for i in range(2)]
                cc_ = [sbt(ph, "cc%d" % i, [128, S_LEN], F32) for i in range(3)]
                blk = ph.enter_context(nc.Block())
                dma(SP, gpost[:, :], gpost_f_d[:, l, :], [], ["gpost"])
                for i in range(2):
                    S.op(POOL, lambda e, u=ur[i]: e.memset(u[:, 0:2], 0.0), [], ["ur%d" % i])
                ubank = Rot([0, 1, 2, 3, 4, 5])
                dbank = Rot([6, 7])
                evq = Rot([ACT, DVE])

                def up_chunk(f):
                    wi = f % 2
                    wk = "Wu%d" % wi
                    wload(Wu[wi][:, :, 0:128], w_up_d[l][:, f * 128:(f + 1) * 128].rearrange("(kc p) c -> p kc c", p=128),
                          [], [wk])
                    wload(Wu[wi][:, :, 128:256],
                          w_up_d[l][:, D_FF + f * 128:D_FF + (f + 1) * 128].rearrange("(kc p) c -> p kc c", p=128), [], [wk])
                    gi_ = 0 if f % 2 == 0 else 2
                    for which in range(2):
                        uk = "ur%d" % which
                        ci = gi_ if which == 0 else 1
                        ck = "cc%d" % ci
                        for tb in range(4):
                            b = ubank.next()
                            for kc in range(KC):
                                mm(ps[b][:, :], Wu[wi][:, kc, which * 128:(which + 1) * 128],
                                   hT[:, kc, tb * 512:(tb + 1) * 512], kc == 0, kc == KC - 1, [wk, "hT%d" % kc], [pk[b]])
                            copy_on(ACT, ur[which][:, 2 + tb * 512:2 + (tb + 1) * 512], ps[b][:, :], [pk[b]], [uk])
                        chn = which * 32 + f
                        wcol = lambda w: convw[:, (l * 3 + w) * 64 + chn:(l * 3 + w) * 64 + chn + 1]
                        bcol = convb[:, l * 64 + chn:l * 64 + chn + 1]
                        ts(DVE, cc_[ci][:, :], ur[which][:, 2:2 + S_LEN], wcol(2), bcol, ALU.mult, ALU.add,
                           [uk, "convw", "convb"], [ck])
                        stt(DVE, cc_[ci][:, :], ur[which][:, 1:1 + S_LEN], wcol(1), cc_[ci][:, :], ALU.mult, ALU.add,
                            [uk, "convw", ck], [ck])
                        stt(DVE, cc_[ci][:, :], ur[which][:, 0:S_LEN], wcol(0), cc_[ci][:, :], ALU.mult, ALU.add,
                            [uk, "convw", ck], [ck])
                    gk = "cc%d" % gi_
                    act(cc_[gi_][:, :], cc_[gi_][:, :], AF.Gelu_apprx_tanh, [gk], [gk])
                    def fin(f=f, gi_=gi_, gk=gk):
                        tt(POOL, aT[:, f % 4, :], cc_[gi_][:, :], cc_[1][:, :], ALU.mult, [gk, "cc1"], ["aT%d" % (f % 4)])
                    return fin

                def down_block(e8):
                    wi = 0
                    for t in range(NT):
                        for half in range(2):
                            b = dbank.next()
                            for c in range(4):
                                mm(ps[b][:, :], aT[:, c, t * 128:(t + 1) * 128], Wd[wi][:, c, half * 512:(half + 1) * 512],
                                   c == 0, c == 3, ["aT%d" % c, "Wd%d" % wi], [pk[b]])
                            o = Y[:, t, half * 512:(half + 1) * 512]
                            if e8 == 0:
                                copy_on(DVE, o, ps[b][:, :], [pk[b]], ["Y%d" % t])
                            else:
                                tt(DVE, o, ps[b][:, :], o, ALU.add, [pk[b], "Y%d" % t], ["Y%d" % t])

                for e8 in range(8):
                    for c in range(4):
                        fin = up_chunk(e8 * 4 + c)
                        if c == 0:
                            if e8 > 0:
                                down_block(e8 - 1)
                            wload(Wd[0][:, :, :],
                                  w_down_d[l][e8 * 512:(e8 + 1) * 512, :].rearrange("(c p) n -> p c n", p=128),
                                  [], ["Wd0"])
                        fin()
                down_block(7)
                last = (l == DEPTH - 1)
                for t in range(NT):
                    par = t % 2
                    ys = [(Y[:, t, :], "Y%d" % t, slice(0, D))]
                    post_residual(t, par, ys, out_d, gpa, (l + 1) * KC if not last else 0, last)
                S.emit(blk, last=last)
    return nc


def _t5_bucket(dist):
    max_exact = 16
    nf = np.maximum(dist, 1).astype(np.float32)
    large = max_exact + (np.log(nf / np.float32(max_exact)) / np.float32(np.log(2048 / max_exact))
                         * np.float32(32 - max_exact)).astype(np.int32)
    large = np.minimum(large, 31)
    return np.where(dist < max_exact, dist, large)


def _host_constants(rel_bias):
    b = np.arange(128)[:, None]
    a = np.arange(128)[None, :]
    dist_diag = a - b
    dist_prev = a + 128 - b
    dist = np.concatenate([dist_diag, dist_prev], axis=1)
    distc = np.maximum(dist, 0)
    raw = np.zeros((128, 20, 256), np.float32)
    for h in range(20):
        d = A_GROUPS[h // 4][1] if h < 12 else 1
        idx = _t5_bucket(distc * d)
        raw[:, h, :] = rel_bias[idx, h]
    mask = np.zeros((128, 2, 256), np.float32)
    for mi, maxd in enumerate((128, 127)):
        ok = (dist >= 0) & (dist <= maxd)
        mask[:, mi, :] = np.where(ok, 0.0, NEG)
    s_ = np.arange(128)[:, None]
    t_ = np.arange(128)[None, :]
    cm01 = (s_ < t_).astype(np.float32)
    cmneg = np.where(s_ < t_, 0.0, NEG).astype(np.float32)
    tri = np.where(s_ >= t_, -8.0, 0.0).astype(np.float32)
    ones = np.full((128, 128), -8.0, np.float32)
    ident = np.eye(128, dtype=np.float32)
    return dict(bias_raw=raw.reshape(128, -1), bias_mask=mask.reshape(128, -1), cmask01=cm01, cmaskneg=cmneg,
                trineg8=tri, onesneg8=ones, ident=ident)


def _chunked(v):
    L, n = v.shape[0], v.shape[1] // 128
    return np.ascontiguousarray(v.reshape(L, n, 128).transpose(2, 0, 1).reshape(128, L * n))


_NC_CACHE = {}


def kernel(x, rel_bias, attn_pre_norm, w_in, b_gate, sinks, w_br_a, w_br_b, w_br_c, w_out,
           attn_post_norm, ffn_pre_norm, w_up, conv_w, conv_b, w_down, ffn_post_norm, _dbg=None, _stop=None, _ncores=None):
    f = lambda a: np.ascontiguousarray(np.asarray(a, dtype=np.float32))
    x = f(x)
    shared = dict(
        w_in=f(w_in), w_br=np.ascontiguousarray(np.concatenate([f(w_br_a), f(w_br_b), f(w_br_c)], axis=1)),
        w_out=f(w_out), w_up=f(w_up), w_down=f(w_down),
        gpre_a=_chunked(f(attn_pre_norm)), gpre_f=_chunked(f(ffn_pre_norm)),
        gpost_a=np.ascontiguousarray(np.broadcast_to(f(attn_post_norm)[None], (128, DEPTH, D))),
        gpost_f=np.ascontiguousarray(np.broadcast_to(f(ffn_post_norm)[None], (128, DEPTH, D))),
        bgate=_chunked(f(b_gate).reshape(DEPTH, 3 * D)),
        sinks_b=np.ascontiguousarray(np.broadcast_to(f(sinks).reshape(1, DEPTH * 8), (128, DEPTH * 8))),
        convw=_chunked(f(conv_w).reshape(DEPTH, 3 * 2 * D_FF)),
        convb=_chunked(f(conv_b)),
    )
    shared.update(_host_constants(f(rel_bias)))
    nc = build_program(_dbg, _stop)
    n = x.shape[0] if _ncores is None else _ncores
    in_maps = [dict(shared, x=np.ascontiguousarray(x[i])) for i in range(n)]
    res = run_bass_kernel_spmd(nc, in_maps, core_ids=list(range(n)))
    out = np.stack([np.asarray(r["out"], dtype=np.float32) for r in res.results], axis=0)
    if _dbg is not None or _stop is not None:
        return out, [np.asarray(r["dbg"]) for r in res.results]
    return out
```

```python
import numpy as np
from contextlib import ExitStack
import ml_dtypes
import concourse.bass as bass
import concourse.mybir as mybir
from concourse.bass_utils import run_bass_kernel_spmd

F32 = mybir.dt.float32
BF16 = mybir.dt.bfloat16
AF = mybir.ActivationFunctionType
ALU = mybir.AluOpType

PE, ACT, DVE, POOL, SP = "pe", "act", "dve", "pool", "sp"
ENGS = (PE, ACT, DVE, POOL, SP)
N_DMA_SEMS = 10

S_LEN = 2048
D = 1024
DEPTH = 2
NT = 16
KC = 8
SCALE = 0.125
EPS = 1e-6
NEG = -30000.0
A_GROUPS = ((128, 1), (512, 4), (2048, 16))
OFF_B_Q = 2304
OFF_B_KV = 2816
OFF_C = 3072
OFF_GATE = 3840
IN_COLS = 6912
D_FF = 4096


class Sched:
    def __init__(self, nc, stack):
        self.nc = nc
        self.ops = {e: [] for e in ENGS}
        self.cnt = {e: 0 for e in ENGS}
        self.sem = {e: stack.enter_context(nc.semaphore("s_" + e)) for e in (PE, ACT, DVE, POOL)}
        self.dsem = {e: [stack.enter_context(nc.semaphore("d_%s%d" % (e, i))) for i in range(N_DMA_SEMS)]
                     for e in (SP, ACT, POOL)}
        self.dcnt = {e: [0] * N_DMA_SEMS for e in (SP, ACT, POOL)}
        self.drr = {e: 0 for e in (SP, ACT, POOL)}
        self.known = {e: {} for e in ENGS}
        self.lastw = {}
        self.readers = {}
        self.final_tokens = []

    def _need(self, eng, tok, waits):
        if tok is None:
            return
        sem, val, src = tok
        if eng == PE and src == PE:
            return
        k = id(sem)
        if self.known[eng].get(k, 0) >= val:
            return
        self.known[eng][k] = val
        for i, (s, v) in enumerate(waits):
            if s is sem:
                waits[i] = (s, max(v, val))
                return
        waits.append((sem, val))

    def _deps(self, eng, reads, writes):
        waits = []
        for k in reads:
            self._need(eng, self.lastw.get(k), waits)
        for k in writes:
            self._need(eng, self.lastw.get(k), waits)
            for t in self.readers.get(k, ()):
                self._need(eng, t, waits)
        return waits

    def _commit(self, tok, reads, writes):
        for k in reads:
            self.readers.setdefault(k, []).append(tok)
        for k in writes:
            self.lastw[k] = tok
            self.readers[k] = []

    def op(self, eng, fn, reads=(), writes=(), after=None):
        pw = [k for k in reads if k.startswith("ps")]
        if pw:
            reads = [k for k in reads if not k.startswith("ps")]
            writes = list(writes) + pw
        waits = self._deps(eng, reads, writes)
        if after is not None:
            self._need(eng, (after[0], after[1], "forced"), waits)
        self.cnt[eng] += 1
        tok = (self.sem[eng], self.cnt[eng], eng)
        self.ops[eng].append((waits, fn, self.sem[eng], 1))
        self._commit(tok, reads, writes)
        return tok

    def dma(self, q, fn, reads=(), writes=(), final=False):
        waits = self._deps(q, reads, writes)
        i = self.drr[q]
        self.drr[q] = (i + 1) % N_DMA_SEMS
        if self.dcnt[q][i] > 0:
            self._need(q, (self.dsem[q][i], self.dcnt[q][i], "dma"), waits)
        self.dcnt[q][i] += 16
        tok = (self.dsem[q][i], self.dcnt[q][i], "dma")
        self.ops[q].append((waits, fn, self.dsem[q][i], 16))
        self._commit(tok, reads, writes)
        if final:
            self.final_tokens.append(tok)
        return tok

    def emit(self, block, last=False):
        fw = []
        if last:
            for t in self.final_tokens:
                self._need(SP, t, fw)
        ops = self.ops
        self.ops = {e: [] for e in ENGS}

        def run(e, lst, extra=()):
            for waits, fn, sem, amt in lst:
                for s, v in waits:
                    e.wait_ge(s, v)
                fn(e).then_inc(sem, amt)
            for s, v in extra:
                e.wait_ge(s, v)

        @block.tensor
        def _(e):
            run(e, ops[PE])

        @block.scalar
        def _(e):
            run(e, ops[ACT])

        @block.vector
        def _(e):
            run(e, ops[DVE])

        @block.gpsimd
        def _(e):
            run(e, ops[POOL])

        @block.sync
        def _(e):
            run(e, ops[SP], fw)


class Rot:
    def __init__(self, items):
        self.items = list(items)
        self.i = 0

    def next(self):
        v = self.items[self.i % len(self.items)]
        self.i += 1
        return v


def build_program(dbg=None, stop=None):
    nc = bass.Bass("TRN2", target_bir_lowering=False)

    def din(name, shape, dt=F32):
        return nc.dram_tensor(name, list(shape), dt, kind="ExternalInput").ap()

    x_d = din("x", [S_LEN, D])
    w_in_d = din("w_in", [DEPTH, D, IN_COLS])
    w_br_d = din("w_br", [DEPTH, D, D])
    w_out_d = din("w_out", [DEPTH, D, D])
    w_up_d = din("w_up", [DEPTH, D, 2 * D_FF])
    w_down_d = din("w_down", [DEPTH, D_FF, D])
    gpa_d = din("gpre_a", [128, DEPTH * KC])
    gpf_d = din("gpre_f", [128, DEPTH * KC])
    gpost_a_d = din("gpost_a", [128, DEPTH, D])
    gpost_f_d = din("gpost_f", [128, DEPTH, D])
    bgate_d = din("bgate", [128, DEPTH * 24])
    sinks_d = din("sinks_b", [128, DEPTH * 8])
    convw_d = din("convw", [128, DEPTH * 3 * 64])
    convb_d = din("convb", [128, DEPTH * 64])
    braw_d = din("bias_raw", [128, 20 * 256])
    bmask_d = din("bias_mask", [128, 2 * 256])
    cmask01_d = din("cmask01", [128, 128])
    cmaskneg_d = din("cmaskneg", [128, 128])
    trineg_d = din("trineg8", [128, 128])
    onesneg_d = din("onesneg8", [128, 128])
    ident_d = din("ident", [128, 128])
    out_d = nc.dram_tensor("out", [S_LEN, D], F32, kind="ExternalOutput").ap()
    dbg_d = None
    if dbg is not None or stop is not None:
        dbg_d = nc.dram_tensor("dbg", [128, 8 * 2048], F32, kind="ExternalOutput").ap()

    with ExitStack() as top:
        S = Sched(nc, top)

        uid = [0]

        def sbt(st, name, shape, dt=F32):
            uid[0] += 1
            return st.enter_context(nc.sbuf_tensor("sb%d_%s" % (uid[0], name), list(shape), dt))

        ps = [top.enter_context(nc.psum_tensor("ps%d" % i, [128, 512], F32)) for i in range(8)]
        pk = ["ps%d" % i for i in range(8)]

        hT = sbt(top, "hT", [128, KC, S_LEN], BF16)
        ident = sbt(top, "ident", [128, 128], BF16)
        biasT = sbt(top, "biasT", [128, 20, 256], BF16)
        cmask01 = sbt(top, "cmask01", [128, 128], F32)
        cmaskneg = sbt(top, "cmaskneg", [128, 128], BF16)
        trineg = sbt(top, "trineg", [128, 128], BF16)
        onesneg = sbt(top, "onesneg", [128, 128], BF16)
        gpa = sbt(top, "gpa", [128, DEPTH * KC])
        gpf = sbt(top, "gpf", [128, DEPTH * KC])
        bgate = sbt(top, "bgate", [128, DEPTH * 24])
        esink = sbt(top, "esink", [128, DEPTH * 8])
        convw = sbt(top, "convw", [128, DEPTH * 3 * 64])
        convb = sbt(top, "convb", [128, DEPTH * 64])
        gpost = sbt(top, "gpost", [128, D])
        xa = [sbt(top, "xa%d" % i, [128, D]) for i in range(2)]
        xb1 = sbt(top, "xb", [128, D])
        xb = [xb1, xb1]
        xn = [sbt(top, "xn%d" % i, [128, D], BF16) for i in range(2)]
        sq = sbt(top, "sqjunk", [128, D], BF16)
        stat = [sbt(top, "stat%d" % i, [128, 8]) for i in range(2)]

        def mm(out, lhsT, rhs, start, stop, reads, writes, after=None):
            return S.op(PE, lambda e, o=out, l=lhsT, r=rhs, a=start, b=stop: e.matmul(o, lhsT=l, rhs=r, start=a, stop=b),
                        reads, writes, after=after)

        def act(out, in_, func, reads, writes, **kw):
            S.op(ACT, lambda e, o=out, i=in_, f=func, k=kw: e.activation(out=o, in_=i, func=f, **k), reads, writes)

        def copy_on(eng, out, in_, reads, writes):
            if eng == ACT:
                act(out, in_, AF.Copy, reads, writes)
            else:
                S.op(eng, lambda e, o=out, i=in_: e.tensor_copy(out=o, in_=i), reads, writes)

        def tt(eng, out, in0, in1, op, reads, writes):
            S.op(eng, lambda e, o=out, a=in0, b=in1, p=op: e.tensor_tensor(out=o, in0=a, in1=b, op=p), reads, writes)

        def ts(eng, out, in0, s1, s2, op0, op1, reads, writes):
            if s2 is None:
                S.op(eng, lambda e, o=out, a=in0, x=s1, p=op0: e.tensor_scalar(out=o, in0=a, scalar1=x, scalar2=None, op0=p),
                     reads, writes)
            else:
                S.op(eng, lambda e, o=out, a=in0, x=s1, y=s2, p=op0, q=op1:
                     e.tensor_scalar(out=o, in0=a, scalar1=x, scalar2=y, op0=p, op1=q), reads, writes)

        def stt(eng, out, in0, scalar, in1, op0, op1, reads, writes):
            S.op(eng, lambda e, o=out, a=in0, s=scalar, b=in1, p=op0, q=op1:
                 e.scalar_tensor_tensor(out=o, in0=a, scalar=s, in1=b, op0=p, op1=q), reads, writes)

        def dma(q, out, in_, reads, writes, final=False):
            S.dma(q, lambda e, o=out, i=in_: e.dma_start(out=o, in_=i), reads, writes, final=final)

        def wload(dst, src, reads, writes):
            dma(POOL, dst, src, reads, writes)

        def norm_to_hT(xt_ap, xt_key, gvec, gcol0, t, par):
            st_ = stat[par]
            sk = "stat%d" % par
            S.op(DVE, lambda e, o=st_[:, 0:4]: e.memset(o, 0.0), [], [sk])
            act(sq[:, :], xt_ap, AF.Square, [xt_key], ["sq", sk], accum_out=st_[:, 0:1])
            ts(DVE, st_[:, 1:2], st_[:, 0:1], 1.0 / D, EPS, ALU.mult, ALU.add, [sk], [sk])
            act(st_[:, 2:3], st_[:, 1:2], AF.Ln, [sk], [sk])
            act(st_[:, 3:4], st_[:, 2:3], AF.Exp, [sk], [sk], scale=-0.5)
            xk = "xn%d" % par
            ts(DVE, xn[par][:, :], xt_ap, st_[:, 3:4], None, ALU.mult, None, [xt_key, sk], [xk])
            for half in range(2):
                b = tbank.next()
                for c4 in range(4):
                    c = half * 4 + c4
                    mm(ps[b][:, c4 * 128:(c4 + 1) * 128], xn[par][:, c * 128:(c + 1) * 128], ident[:, :], True, True,
                       [xk, "ident"], [pk[b]])
                for c4 in range(4):
                    c = half * 4 + c4
                    eng = DVE
                    ts(eng, hT[:, c, t * 128:(t + 1) * 128], ps[b][:, c4 * 128:(c4 + 1) * 128],
                       gvec[:, gcol0 + c:gcol0 + c + 1], None, ALU.mult, None, [pk[b], "gvec"], ["hT%d" % c])

        def post_residual(t, par, ys, x_src, gnext, gcol0, final):
            rows = slice(t * 128, (t + 1) * 128)
            st_ = stat[par]
            sk = "stat%d" % par
            xk = "xa%d" % par
            dma(SP, xa[par][:, :], x_src[rows, :], ["outd%d" % t], [xk])
            S.op(DVE, lambda e, o=st_[:, 4:8]: e.memset(o, 0.0), [], [sk])
            for i, (yap, ykey, csl) in enumerate(ys):
                n = csl.stop - csl.start
                act(sq[:, 0:n], yap, AF.Square, [ykey], ["sq", sk], accum_out=st_[:, 4 + i:5 + i])
            tt(DVE, st_[:, 6:7], st_[:, 4:5], st_[:, 5:6], ALU.add, [sk], [sk])
            ts(DVE, st_[:, 6:7], st_[:, 6:7], 1.0 / D, EPS, ALU.mult, ALU.add, [sk], [sk])
            act(st_[:, 7:8], st_[:, 6:7], AF.Ln, [sk], [sk])
            act(st_[:, 6:7], st_[:, 7:8], AF.Exp, [sk], [sk], scale=-0.5)
            bk = "xb"
            for i, (yap, ykey, csl) in enumerate(ys):
                stt(DVE, xb[par][:, csl], yap, st_[:, 6:7], gpost[:, csl], ALU.mult, ALU.mult, [ykey, sk, "gpost"], [bk])
            tt(DVE, xa[par][:, :], xa[par][:, :], xb[par][:, :], ALU.add, [xk, bk], [xk])
            dma(SP, out_d[rows, :], xa[par][:, :], [xk], ["outd%d" % t], final=final)
            if not final:
                norm_to_hT(xa[par][:, :], xk, gnext, gcol0, t, par)

        tbank = Rot([4, 5, 6, 7])

        with ExitStack() as ph:
            braw = sbt(ph, "braw", [128, 20, 256])
            bmask = sbt(ph, "bmask", [128, 2, 256])
            sinks = sbt(ph, "sinks", [128, DEPTH * 8])
            blk = ph.enter_context(nc.Block())
            dma(SP, braw[:, :, :], braw_d.rearrange("p (h n) -> p h n", h=20), [], ["braw"])
            dma(SP, bmask[:, :, :], bmask_d.rearrange("p (h n) -> p h n", h=2), [], ["bmask"])
            dma(SP, sinks[:, :], sinks_d, [], ["sinks"])
            dma(SP, cmask01[:, :], cmask01_d, [], ["cmask01"])
            for dst, src, k in ((cmaskneg, cmaskneg_d, "cmaskneg"), (trineg, trineg_d, "trineg"),
                                (onesneg, onesneg_d, "onesneg"), (ident, ident_d, "ident")):
                wload(dst[:, :], src, [], [k])
            dma(SP, gpa[:, :], gpa_d, [], ["gvec"])
            dma(SP, gpf[:, :], gpf_d, [], ["gvec"])
            dma(SP, bgate[:, :], bgate_d, [], ["bgate"])
            dma(SP, convw[:, :], convw_d, [], ["convw"])
            dma(SP, convb[:, :], convb_d, [], ["convb"])
            act(esink[:, :], sinks[:, :], AF.Exp, ["sinks"], ["esink"])
            for h in range(20):
                mi = 0 if h < 12 else 1
                stt(DVE, biasT[:, h, :], braw[:, h, :], 1.0 / SCALE, bmask[:, mi, :], ALU.mult, ALU.add,
                    ["braw", "bmask"], ["biasT"])
            for t in range(NT):
                par = t % 2
                dma(SP, xa[par][:, :], x_d[t * 128:(t + 1) * 128, :], [], ["xa%d" % par])
                norm_to_hT(xa[par][:, :], "xa%d" % par, gpa, 0, t, par)
            if stop == "p0":
                for c in range(KC):
                    dma(POOL, dbg_d[:, c * 2048:(c + 1) * 2048], hT[:, c, :], ["hT%d" % c], [], final=True)
            S.emit(blk, last=(stop == "p0"))
        if stop == "p0":
            return nc

        for l in range(DEPTH):
            with ExitStack() as lay:
                oT = sbt(lay, "oT", [128, 8, S_LEN], BF16)
                hkeys = ["hT%d" % c for c in range(KC)]

                with ExitStack() as ph:
                    QT = [sbt(ph, "QT%d" % i, [128, S_LEN], BF16) for i in range(2)]
                    KT = [sbt(ph, "KT%d" % i, [128, S_LEN], BF16) for i in range(2)]
                    V = [sbt(ph, "V%d" % i, [128, NT, 2, 128], BF16) for i in range(2)]
                    Wt = [sbt(ph, "Wt%d" % i, [128, KC, 384], BF16) for i in range(2)]
                    Pb = [sbt(ph, "Pb%d" % i, [128, 4096], BF16) for i in range(2)]
                    acc = sbt(ph, "acc", [128, 2, S_LEN], F32)
                    rec = [sbt(ph, "rec%d" % i, [128, 512], F32) for i in range(2)]
                    Eb = [sbt(ph, "Eb%d" % i, [128, 512], F32) for i in range(2)]
                    SPb = [sbt(ph, "SPb%d" % i, [128, 512], BF16) for i in range(3)]
                    Wb = [sbt(ph, "Wb%d" % i, [128, 512], BF16) for i in range(3)]
                    RS32 = sbt(ph, "RS32", [128, 512], F32)
                    RSb = [sbt(ph, "RSb%d" % i, [128, 512], BF16) for i in range(2)]
                    blk = ph.enter_context(nc.Block())
                    for i in range(2):
                        S.op(DVE, lambda e, v=V[i]: e.memset(v[:, :, :, 64:128], 1.0), [], ["V%d" % i])

                    sbank = Rot([0, 1])
                    obank = Rot([2, 3])
                    pbank = Rot([4, 5, 6, 7])
                    evq = Rot([DVE, ACT])

                    def proj(job):
                        si = job["set"]
                        wt = Wt[si]
                        wk = "Wt%d" % si
                        src = w_in_d[l]
                        c0 = job["qcol"]
                        wload(wt[:, :, 0:128], src[:, c0:c0 + 128].rearrange("(kc p) c -> p kc c", p=128), [], [wk])
                        for (cs, n, dst) in job["kcols"]:
                            wload(wt[:, :, 128 + dst:128 + dst + n],
                                  src[:, cs:cs + n].rearrange("(kc p) c -> p kc c", p=128), [], [wk])
                        cs, nv = job["vcols"]
                        wload(wt[:, :, 256:256 + nv], src[:, cs:cs + nv].rearrange("(kc p) c -> p kc c", p=128), [], [wk])
                        d = job["d"]
                        L = S_LEN // d
                        nb = L // 128
                        for which, dst, dk in ((0, QT[si], "QT%d" % si), (1, KT[si], "KT%d" % si)):
                            for tb in range(4):
                                b = pbank.next()
                                for kc in range(KC):
                                    if d == 1:
                                        rhs = hT[:, kc, tb * 512:(tb + 1) * 512]
                                        o = ps[b][:, :]
                                    elif d == 4:
                                        rhs = hT[:, kc, tb:S_LEN:4]
                                        o = ps[b][:, :]
                                    else:
                                        rhs = hT[:, kc, :].rearrange("p (m r) -> p r m", r=16)[:, 4 * tb:4 * tb + 4, :]
                                        o = ps[b][:, :].rearrange("p (r m) -> p r m", r=4)
                                    mm(o, wt[:, kc, which * 128:(which + 1) * 128], rhs, kc == 0, kc == KC - 1,
                                       [wk, "hT%d" % kc], [pk[b]])
                                copy_on(evq.next(), dst[:, tb * 512:(tb + 1) * 512], ps[b][:, :], [pk[b]], [dk])
                        vk = "V%d" % si
                        for t4 in range(4):
                            b = pbank.next()
                            for k in range(4):
                                ti = t4 * 4 + k
                                c, j = divmod(ti, nb)
                                start = j * 128 * d + c
                                for kc in range(KC):
                                    mm(ps[b][:, k * 128:k * 128 + nv], hT[:, kc, start:start + 127 * d + 1:d],
                                       wt[:, kc, 256:256 + nv], kc == 0, kc == KC - 1, [wk, "hT%d" % kc], [pk[b]])
                            if nv == 128:
                                o = V[si][:, t4 * 4:t4 * 4 + 4, :, 0:64]
                                i_ = ps[b][:, :].rearrange("p (t h e) -> p t h e", t=4, h=2)
                            else:
                                o = V[si][:, t4 * 4:t4 * 4 + 4, 0, 0:64]
                                i_ = ps[b][:, :].rearrange("p (t e) -> p t e", t=4)[:, :, 0:64]
                            copy_on(evq.next(), o, i_, [pk[b]], [vk])

                    def s_stage(job, head):
                        si = job["set"]
                        hl, hv, bi, pbi = head["hl"], head["hv"], head["bias"], head["pb"]
                        d = job["d"]
                        nb = (S_LEN // d) // 128
                        Wd = 256 if nb > 1 else 128
                        per = 512 // Wd
                        p0 = hl * 64
                        for u in range(NT // per):
                            b = sbank.next()
                            for k in range(per):
                                ti = u * per + k
                                c, j = divmod(ti, nb)
                                n = 256 if (nb > 1 and j < nb - 1) else 128
                                mm(ps[b][:, k * Wd:k * Wd + n], KT[si][p0:p0 + 64, ti * 128:(ti + 1) * 128],
                                   QT[si][p0:p0 + 64, ti * 128:ti * 128 + n], True, False,
                                   ["KT%d" % si, "QT%d" % si], [pk[b]])
                                mm(ps[b][:, k * Wd:k * Wd + n], ident[:, :], biasT[:, bi, 0:n], False, True,
                                   ["ident", "biasT"], [pk[b]])
                            act(Pb[pbi][:, u * 512:(u + 1) * 512], ps[b][:, :], AF.Exp, [pk[b]], ["Pb%d" % pbi],
                                scale=SCALE)

                    def pv_stage(job, head):
                        si = job["set"]
                        hl, hv, pbi = head["hl"], head["hv"], head["pb"]
                        d = job["d"]
                        nb = (S_LEN // d) // 128
                        Wd = 256 if nb > 1 else 128
                        for q4 in range(4):
                            b = obank.next()
                            for k in range(4):
                                qi = q4 * 4 + k
                                c, i = divmod(qi, nb)
                                lst = []
                                if i > 0:
                                    lst.append((qi - 1, 128))
                                lst.append((qi, 0))
                                for idx, (ti, off) in enumerate(lst):
                                    mm(ps[b][:, k * 128:(k + 1) * 128], V[si][:, ti, hv, :],
                                       Pb[pbi][:, ti * Wd + off:ti * Wd + off + 128], idx == 0, idx == len(lst) - 1,
                                       ["V%d" % si, "Pb%d" % pbi], [pk[b]])
                            head["evac"](job, head, q4, b)

                    def evac_A(job, head, q4, b):
                        hl = head["hl"]
                        d = job["d"]
                        gi = job["gi"]
                        ak = "acc%d" % hl
                        if d == 1:
                            o = acc[:, hl, q4 * 512:(q4 + 1) * 512]
                            i_ = ps[b][:, :]
                        elif d == 4:
                            o = acc[:, hl, q4:S_LEN:4]
                            i_ = ps[b][:, :]
                        else:
                            o = acc[:, hl, :].rearrange("p (m r) -> p r m", r=16)[:, 4 * q4:4 * q4 + 4, :]
                            i_ = ps[b][:, :].rearrange("p (r m) -> p r m", r=4)
                        if gi == 0:
                            copy_on(DVE, o, i_, [pk[b]], [ak])
                        else:
                            tt(DVE, o, i_, o, ALU.add, [pk[b], ak], [ak])

                    def finalize_A(pair):
                        for hl in range(2):
                            ak = "acc%d" % hl
                            for tb in range(4):
                                r = tb % 2
                                sl = slice(tb * 512, (tb + 1) * 512)
                                S.op(DVE, lambda e, o=rec[r][0:64, :], i=acc[64:128, hl, sl]: e.reciprocal(out=o, in_=i),
                                     [ak], ["rec%d" % r])
                                tt(DVE, oT[hl * 64:(hl + 1) * 64, pair, sl], acc[0:64, hl, sl], rec[r][0:64, :], ALU.mult,
                                   [ak, "rec%d" % r], ["oT%d" % pair])

                    def evac_B(job, head, q4, b):
                        hl = head["hl"]
                        h = head["h"]
                        r = q4 % 2
                        sl = slice(q4 * 512, (q4 + 1) * 512)
                        col = l * 8 + h
                        ts(DVE, rec[r][64:128, :], ps[b][64:128, :], esink[64:128, col:col + 1], None, ALU.add, None,
                           [pk[b], "esink"], ["rec%d" % r])
                        S.op(DVE, lambda e, o=rec[r][64:128, :]: e.reciprocal(out=o, in_=o), ["rec%d" % r], ["rec%d" % r])
                        ch = 2 + h // 2
                        tt(DVE, oT[hl * 64:(hl + 1) * 64, ch, sl], ps[b][0:64, :], rec[r][64:128, :], ALU.mult,
                           [pk[b], "rec%d" % r], ["oT%d" % ch])

                    jobs = []
                    for pair in range(2):
                        for gi, (win, d) in enumerate(A_GROUPS):
                            base = gi * 256 + pair * 128
                            jobs.append(dict(kind="A", set=len(jobs) % 2, qcol=base, kcols=[(768 + base, 128, 0)],
                                             vcols=(1536 + base, 128), d=d, gi=gi, pair=pair,
                                             heads=[dict(hl=hl, hv=hl, bias=gi * 4 + pair * 2 + hl, evac=evac_A)
                                                    for hl in range(2)]))
                    for pj in range(4):
                        kvh = pj // 2
                        kc0 = OFF_B_KV + kvh * 64
                        jobs.append(dict(kind="B", set=len(jobs) % 2, qcol=OFF_B_Q + pj * 128,
                                         kcols=[(kc0, 64, 0), (kc0, 64, 64)], vcols=(OFF_B_KV + 128 + kvh * 64, 64),
                                         d=1, gi=0, pair=pj,
                                         heads=[dict(hl=hl, hv=0, h=pj * 2 + hl, bias=12 + pj * 2 + hl, evac=evac_B)
                                                for hl in range(2)]))
                    cjobs = []
                    for pair in range(2):
                        cjobs.append(dict(kind="C", set=(len(jobs) + pair) % 2, qcol=OFF_C + pair * 128,
                                          kcols=[(OFF_C + 256 + pair * 128, 128, 0)],
                                          vcols=(OFF_C + 512 + pair * 128, 128), d=1, gi=0, pair=pair, heads=[]))
                    alljobs = jobs + cjobs

                    seq = []
                    for ji, job in enumerate(jobs):
                        for hi, head in enumerate(job["heads"]):
                            head["pb"] = len(seq) % 2
                            seq.append((ji, hi))
                    proj(alljobs[0])
                    prev = None
                    for (ji, hi) in seq:
                        job = jobs[ji]
                        head = job["heads"][hi]
                        s_stage(job, head)
                        if prev is not None:
                            pj_, ph_ = prev
                            pv_stage(jobs[pj_], jobs[pj_]["heads"][ph_])
                            if ph_ == 1 and jobs[pj_]["kind"] == "A" and jobs[pj_]["gi"] == 2:
                                finalize_A(jobs[pj_]["pair"])
                        if hi == 0 and ji + 1 < len(alljobs):
                            proj(alljobs[ji + 1])
                        prev = (ji, hi)
                    pj_, ph_ = prev
                    pv_stage(jobs[pj_], jobs[pj_]["heads"][ph_])
                    proj(cjobs[1])

                    cset = [cjobs[0]["set"], cjobs[1]["set"]]
                    zb = Rot([0, 1])
                    ab = Rot([2, 3])
                    tiles = [(qb, kb) for qb in range(NT) for kb in range(qb, -1, -1)]
                    info = {}

                    def c_st1(t):
                        qb, kb = tiles[t]
                        b = zb.next()
                        ptok = None
                        for h in (0, 2, 1, 3):
                            si = cset[h // 2]
                            p0 = (h % 2) * 64
                            ptok = mm(ps[b][:, h * 128:(h + 1) * 128], KT[si][p0:p0 + 64, kb * 128:(kb + 1) * 128],
                                      QT[si][p0:p0 + 64, qb * 128:(qb + 1) * 128], True, True,
                                      ["KT%d" % si, "QT%d" % si], [pk[b]], after=(ptok if h == 1 else None))
                        e_i = t % 2
                        sp_i = t % 3
                        act(Eb[e_i][:, :], ps[b][:, :], AF.Exp, [pk[b]], ["Eb%d" % e_i], scale=SCALE)
                        if kb == qb:
                            ev = Eb[e_i][:, :].rearrange("p (h n) -> p h n", h=4)
                            S.op(DVE, lambda e, o=ev, m=cmask01[:, :].unsqueeze(1).to_broadcast([128, 4, 128]):
                                 e.tensor_tensor(out=o, in0=o, in1=m, op=ALU.mult), ["Eb%d" % e_i, "cmask01"], ["Eb%d" % e_i])
                        act(SPb[sp_i][:, :], Eb[e_i][:, :], AF.Ln, ["Eb%d" % e_i], ["SPb%d" % sp_i], bias=1.0, scale=1.0)
                        info[t] = dict(sp=sp_i)

                    def c_st2(t):
                        qb, kb = tiles[t]
                        b = ab.next()
                        sp_i = info[t]["sp"]
                        rs_i = info[t - 1]["rs"] if kb < qb else None
                        for h in range(4):
                            si = cset[h // 2]
                            p0 = (h % 2) * 64
                            o = ps[b][:, h * 128:(h + 1) * 128]
                            hs = slice(h * 128, (h + 1) * 128)
                            last = "tri"
                            if kb < qb:
                                last = "ones"
                            if kb == qb:
                                last = "mask"
                            mm(o, KT[si][p0:p0 + 64, kb * 128:(kb + 1) * 128], QT[si][p0:p0 + 64, qb * 128:(qb + 1) * 128],
                               True, False, ["KT%d" % si, "QT%d" % si], [pk[b]])
                            mm(o, trineg[:, :], SPb[sp_i][:, hs], False, last == "tri", ["trineg", "SPb%d" % sp_i], [pk[b]])
                            if kb < qb:
                                mm(o, onesneg[:, :], RSb[rs_i][:, hs], False, True, ["onesneg", "RSb%d" % rs_i], [pk[b]])
                            if kb == qb:
                                mm(o, ident[:, :], cmaskneg[:, :], False, True, ["ident", "cmaskneg"], [pk[b]])
                        w_i = t % 3
                        act(Wb[w_i][:, :], ps[b][:, :], AF.Exp, [pk[b]], ["Wb%d" % w_i], scale=SCALE)
                        info[t]["w"] = w_i
                        if kb > 0:
                            if kb == qb:
                                copy_on(DVE, RS32[:, :], SPb[sp_i][:, :], ["SPb%d" % sp_i], ["RS32"])
                            else:
                                tt(DVE, RS32[:, :], RS32[:, :], SPb[sp_i][:, :], ALU.add, ["RS32", "SPb%d" % sp_i], ["RS32"])
                            r_i = t % 2
                            copy_on(DVE, RSb[r_i][:, :], RS32[:, :], ["RS32"], ["RSb%d" % r_i])
                            info[t]["rs"] = r_i

                    def c_st3(t):
                        qb, kb = tiles[t]
                        w_i = info[t]["w"]
                        ob = 4 + (qb % 2) * 2
                        for h in range(4):
                            si = cset[h // 2]
                            hl = h % 2
                            b = ob + h // 2
                            mm(ps[b][hl * 64:(hl + 1) * 64, 0:128], V[si][:, kb, hl, 0:64], Wb[w_i][:, h * 128:(h + 1) * 128],
                               kb == qb, kb == 0, ["V%d" % si, "Wb%d" % w_i], [pk[b]])
                        if kb == 0:
                            for pr in range(2):
                                b = ob + pr
                                copy_on(DVE if pr == 0 else ACT, oT[:, 6 + pr, qb * 128:(qb + 1) * 128], ps[b][:, 0:128],
                                        [pk[b]], ["oT%d" % (6 + pr)])

                    ntl = len(tiles)
                    for i in range(ntl + 2):
                        if i < ntl:
                            c_st1(i)
                        if 0 <= i - 1 < ntl:
                            c_st2(i - 1)
                        if 0 <= i - 2 < ntl:
                            c_st3(i - 2)
                    if stop == "p1":
                        for c in range(8):
                            dma(POOL, dbg_d[:, c * 2048:(c + 1) * 2048], oT[:, c, :], ["oT%d" % c], [], final=True)
                    S.emit(blk, last=(stop == "p1"))
                if stop == "p1":
                    return nc

                with ExitStack() as ph:
                    mT = sbt(ph, "mT", [128, KC, S_LEN], BF16)
                    Wg = [sbt(ph, "Wg%d" % i, [128, KC, 384], BF16) for i in range(2)]
                    Wbr = sbt(ph, "Wbr", [128, KC, D], BF16)
                    Wo = sbt(ph, "Wo", [128, KC, D], BF16)
                    sg = [sbt(ph, "sg%d" % i, [128, 512]) for i in range(3)]
                    tmpm = [sbt(ph, "tmpm%d" % i, [128, 512]) for i in range(3)]
                    blk = ph.enter_context(nc.Block())
                    wload(Wbr[:, :, :], w_br_d[l].rearrange("(kc p) c -> p kc c", p=128), [], ["Wbr"])
                    wload(Wo[:, :, :], w_out_d[l].rearrange("(kc p) c -> p kc c", p=128), [], ["Wo"])
                    dma(SP, gpost[:, :], gpost_a_d[:, l, :], [], ["gpost"])
                    gbank = Rot([0, 1, 2])
                    bbank = Rot([3, 4, 5])
                    brch = ((0, 2), (2, 6), (6, 8))
                    ui = 0
                    for dc in range(KC):
                        wi = dc % 2
                        wk = "Wg%d" % wi
                        for br in range(3):
                            c0 = OFF_GATE + br * 1024 + dc * 128
                            wload(Wg[wi][:, :, br * 128:(br + 1) * 128],
                                  w_in_d[l][:, c0:c0 + 128].rearrange("(kc p) c -> p kc c", p=128), [], [wk])
                        for tb in range(4):
                            tsl = slice(tb * 512, (tb + 1) * 512)
                            for br in range(3):
                                gb = gbank.next()
                                for kc in range(KC):
                                    mm(ps[gb][:, :], Wg[wi][:, kc, br * 128:(br + 1) * 128], hT[:, kc, tsl], kc == 0,
                                       kc == KC - 1, [wk, "hT%d" % kc], [pk[gb]])
                                bb = bbank.next()
                                c_lo, c_hi = brch[br]
                                for ch in range(c_lo, c_hi):
                                    mm(ps[bb][:, :], Wbr[:, ch, dc * 128:(dc + 1) * 128], oT[:, ch, tsl], ch == c_lo,
                                       ch == c_hi - 1, ["Wbr", "oT%d" % ch], [pk[bb]])
                                si_ = ui % 3
                                ui += 1
                                bcol = l * 24 + br * 8 + dc
                                act(sg[si_][:, :], ps[gb][:, :], AF.Sigmoid, [pk[gb], "bgate"], ["sg%d" % si_],
                                    bias=bgate[:, bcol:bcol + 1], scale=1.0)
                                if br == 0:
                                    tt(DVE, tmpm[0][:, :], ps[bb][:, :], sg[si_][:, :], ALU.mult, [pk[bb], "sg%d" % si_], ["tm0"])
                                elif br == 1:
                                    tt(DVE, tmpm[1][:, :], ps[bb][:, :], sg[si_][:, :], ALU.mult, [pk[bb], "sg%d" % si_], ["tm1"])
                                    tt(DVE, tmpm[0][:, :], tmpm[0][:, :], tmpm[1][:, :], ALU.add, ["tm0", "tm1"], ["tm0"])
                                else:
                                    tt(DVE, tmpm[2][:, :], ps[bb][:, :], sg[si_][:, :], ALU.mult, [pk[bb], "sg%d" % si_], ["tm2"])
                                    tt(DVE, mT[:, dc, tsl], tmpm[0][:, :], tmpm[2][:, :], ALU.add, ["tm0", "tm2"], ["mT%d" % dc])
                    ybank = Rot([6, 7])
                    x_src = x_d if l == 0 else out_d
                    for t in range(NT):
                        par = t % 2
                        rows = slice(t * 128, (t + 1) * 128)
                        ys = []
                        for half in range(2):
                            yb = ybank.next()
                            for kc in range(KC):
                                mm(ps[yb][:, :], mT[:, kc, rows], Wo[:, kc, half * 512:(half + 1) * 512], kc == 0, kc == KC - 1,
                                   ["mT%d" % kc, "Wo"], [pk[yb]])
                            ys.append((ps[yb][:, :], pk[yb], slice(half * 512, (half + 1) * 512)))
                        post_residual(t, par, ys, x_src, gpf, l * KC, False)
                    if stop == "p2":
                        for c in range(KC):
                            dma(POOL, dbg_d[:, c * 2048:(c + 1) * 2048], hT[:, c, :], ["hT%d" % c], [], final=True)
                    S.emit(blk, last=(stop == "p2"))
                if stop == "p2":
                    return nc

            with ExitStack() as ph:
                Y = sbt(ph, "Y", [128, NT, D], F32)
                aT = sbt(ph, "aT", [128, 4, S_LEN], BF16)
                Wu = [sbt(ph, "Wu%d" % i, [128, KC, 256], BF16) for i in range(2)]
                Wd = [sbt(ph, "Wd0", [128, 4, D], BF16)]
                ur = [sbt(ph, "ur%d" % i, [128, 2 + S_LEN], F32) for i in range(2)]
                cc_ = [sbt(ph, "cc%d" % i, [128, S_LEN], F32) for i in range(3)]
                blk = ph.enter_context(nc.Block())
                dma(SP, gpost[:, :], gpost_f_d[:, l, :], [], ["gpost"])
                for i in range(2):
                    S.op(DVE, lambda e, u=ur[i]: e.memset(u[:, 0:2], 0.0), [], ["ur%d" % i])
                ubank = Rot([0, 1, 2, 3, 4, 5])
                dbank = Rot([6, 7])
                evq = Rot([ACT, DVE])

                def up_chunk(f):
                    wi = f % 2
                    wk = "Wu%d" % wi
                    wload(Wu[wi][:, :, 0:128], w_up_d[l][:, f * 128:(f + 1) * 128].rearrange("(kc p) c -> p kc c", p=128),
                          [], [wk])
                    wload(Wu[wi][:, :, 128:256],
                          w_up_d[l][:, D_FF + f * 128:D_FF + (f + 1) * 128].rearrange("(kc p) c -> p kc c", p=128), [], [wk])
                    gi_ = 0 if f % 2 == 0 else 2
                    for which in range(2):
                        uk = "ur%d" % which
                        ci = gi_ if which == 0 else 1
                        ck = "cc%d" % ci
                        for tb in range(4):
                            b = ubank.next()
                            for kc in range(KC):
                                mm(ps[b][:, :], Wu[wi][:, kc, which * 128:(which + 1) * 128],
                                   hT[:, kc, tb * 512:(tb + 1) * 512], kc == 0, kc == KC - 1, [wk, "hT%d" % kc], [pk[b]])
                            copy_on(ACT, ur[which][:, 2 + tb * 512:2 + (tb + 1) * 512], ps[b][:, :], [pk[b]], [uk])
                        chn = which * 32 + f
                        wcol = lambda w: convw[:, (l * 3 + w) * 64 + chn:(l * 3 + w) * 64 + chn + 1]
                        bcol = convb[:, l * 64 + chn:l * 64 + chn + 1]
                        ts(DVE, cc_[ci][:, :], ur[which][:, 2:2 + S_LEN], wcol(2), bcol, ALU.mult, ALU.add,
                           [uk, "convw", "convb"], [ck])
                        stt(DVE, cc_[ci][:, :], ur[which][:, 1:1 + S_LEN], wcol(1), cc_[ci][:, :], ALU.mult, ALU.add,
                            [uk, "convw", ck], [ck])
                        stt(DVE, cc_[ci][:, :], ur[which][:, 0:S_LEN], wcol(0), cc_[ci][:, :], ALU.mult, ALU.add,
                            [uk, "convw", ck], [ck])
                    gk = "cc%d" % gi_
                    act(cc_[gi_][:, :], cc_[gi_][:, :], AF.Gelu_apprx_tanh, [gk], [gk])
                    def fin(f=f, gi_=gi_, gk=gk):
                        tt(DVE, aT[:, f % 4, :], cc_[gi_][:, :], cc_[1][:, :], ALU.mult, [gk, "cc1"], ["aT%d" % (f % 4)])
                    return fin

                def down_block(e8):
                    wi = 0
                    for t in range(NT):
                        for half in range(2):
                            b = dbank.next()
                            for c in range(4):
                                mm(ps[b][:, :], aT[:, c, t * 128:(t + 1) * 128], Wd[wi][:, c, half * 512:(half + 1) * 512],
                                   c == 0, c == 3, ["aT%d" % c, "Wd%d" % wi], [pk[b]])
                            o = Y[:, t, half * 512:(half + 1) * 512]
                            if e8 == 0:
                                copy_on(DVE, o, ps[b][:, :], [pk[b]], ["Y%d" % t])
                            else:
                                tt(DVE, o, ps[b][:, :], o, ALU.add, [pk[b], "Y%d" % t], ["Y%d" % t])

                for e8 in range(8):
                    for c in range(4):
                        fin = up_chunk(e8 * 4 + c)
                        if c == 0:
                            if e8 > 0:
                                down_block(e8 - 1)
                            wload(Wd[0][:, :, :],
                                  w_down_d[l][e8 * 512:(e8 + 1) * 512, :].rearrange("(c p) n -> p c n", p=128),
                                  [], ["Wd0"])
                        fin()
                down_block(7)
                last = (l == DEPTH - 1)
                for t in range(NT):
                    par = t % 2
                    ys = [(Y[:, t, :], "Y%d" % t, slice(0, D))]
                    post_residual(t, par, ys, out_d, gpa, (l + 1) * KC if not last else 0, last)
                S.emit(blk, last=last)
    return nc


def _t5_bucket(dist):
    max_exact = 16
    nf = np.maximum(dist, 1).astype(np.float32)
    large = max_exact + (np.log(nf / np.float32(max_exact)) / np.float32(np.log(2048 / max_exact))
                         * np.float32(32 - max_exact)).astype(np.int32)
    large = np.minimum(large, 31)
    return np.where(dist < max_exact, dist, large)


def _host_constants(rel_bias):
    b = np.arange(128)[:, None]
    a = np.arange(128)[None, :]
    dist_diag = a - b
    dist_prev = a + 128 - b
    dist = np.concatenate([dist_diag, dist_prev], axis=1)
    distc = np.maximum(dist, 0)
    raw = np.zeros((128, 20, 256), np.float32)
    for h in range(20):
        d = A_GROUPS[h // 4][1] if h < 12 else 1
        idx = _t5_bucket(distc * d)
        raw[:, h, :] = rel_bias[idx, h]
    mask = np.zeros((128, 2, 256), np.float32)
    for mi, maxd in enumerate((128, 127)):
        ok = (dist >= 0) & (dist <= maxd)
        mask[:, mi, :] = np.where(ok, 0.0, NEG)
    s_ = np.arange(128)[:, None]
    t_ = np.arange(128)[None, :]
    cm01 = (s_ < t_).astype(np.float32)
    cmneg = np.where(s_ < t_, 0.0, NEG).astype(np.float32)
    tri = np.where(s_ >= t_, -8.0, 0.0).astype(np.float32)
    ones = np.full((128, 128), -8.0, np.float32)
    ident = np.eye(128, dtype=np.float32)
    return dict(bias_raw=raw.reshape(128, -1), bias_mask=mask.reshape(128, -1), cmask01=cm01, cmaskneg=cmneg,
                trineg8=tri, onesneg8=ones, ident=ident)


def _chunked(v):
    L, n = v.shape[0], v.shape[1] // 128
    return np.ascontiguousarray(v.reshape(L, n, 128).transpose(2, 0, 1).reshape(128, L * n))


_NC_CACHE = {}


def kernel(x, rel_bias, attn_pre_norm, w_in, b_gate, sinks, w_br_a, w_br_b, w_br_c, w_out,
           attn_post_norm, ffn_pre_norm, w_up, conv_w, conv_b, w_down, ffn_post_norm, _dbg=None, _stop=None, _ncores=None):
    f = lambda a: np.ascontiguousarray(np.asarray(a, dtype=np.float32))
    x = f(x)
    shared = dict(
        w_in=f(w_in), w_br=np.ascontiguousarray(np.concatenate([f(w_br_a), f(w_br_b), f(w_br_c)], axis=1)),
        w_out=f(w_out), w_up=f(w_up), w_down=f(w_down),
        gpre_a=_chunked(f(attn_pre_norm)), gpre_f=_chunked(f(ffn_pre_norm)),
        gpost_a=np.ascontiguousarray(np.broadcast_to(f(attn_post_norm)[None], (128, DEPTH, D))),
        gpost_f=np.ascontiguousarray(np.broadcast_to(f(ffn_post_norm)[None], (128, DEPTH, D))),
        bgate=_chunked(f(b_gate).reshape(DEPTH, 3 * D)),
        sinks_b=np.ascontiguousarray(np.broadcast_to(f(sinks).reshape(1, DEPTH * 8), (128, DEPTH * 8))),
        convw=_chunked(f(conv_w).reshape(DEPTH, 3 * 2 * D_FF)),
        convb=_chunked(f(conv_b)),
    )
    shared.update(_host_constants(f(rel_bias)))
    nc = build_program(_dbg, _stop)
    n = x.shape[0] if _ncores is None else _ncores
    in_maps = [dict(shared, x=np.ascontiguousarray(x[i])) for i in range(n)]
    res = run_bass_kernel_spmd(nc, in_maps, core_ids=list(range(n)))
    out = np.stack([np.asarray(r["out"], dtype=np.float32) for r in res.results], axis=0)
    if _dbg is not None or _stop is not None:
        return out, [np.asarray(r["dbg"]) for r in res.results]
    return out
```
